# Optimizing a Trainium2 kernel written in Bass

```python
import jax, jax.numpy as jnp
from jax import lax
import numpy as np

D_MODEL = 4096
BATCH = 4
SEQ = 2048
DEPTH = 2
DEC_BATCH = 8
DEC_SEQ = 32
PAST_LEN = 4096

CHUNK = 64
D_PLE = 256
EPS = 1e-6
TINY = 1e-30
GLA_H = 4
GLA_DK = 128
GLA_DV = 256
GLA_RANK = 16
GLA_GATE_NORM = 16.0
GDN_H = 16
GDN_DK = 128
GDN_DV = 128
GDN_CONV = 4
HG_H = 8
HG_DK = 128
HG_DV = 128
GLA_KW = GLA_H * GLA_DK
GLA_VW = GLA_H * GLA_DV
GDN_KW = GDN_H * GDN_DK
GDN_VW = GDN_H * GDN_DV
GDN_QKV = 2 * GDN_KW + GDN_VW
HG_KW = HG_H * HG_DK
HG_VW = HG_H * HG_DV
D_MIX = GLA_VW + GDN_VW + HG_VW
D_FF = 11008
FFN_CONV = 3
IN_SPLITS = (GLA_KW, GLA_KW, GLA_VW, GLA_VW, GLA_RANK,
             GDN_QKV, GDN_VW, GDN_H, GDN_H,
             HG_KW, HG_KW, HG_VW, HG_VW)
N_IN = sum(IN_SPLITS)

kernel_name = "hybrid_gla_gdn_hgrn2_stream_step"


def rms_norm(x, w):
    x32 = x.astype(jnp.float32)
    y = x32 * lax.rsqrt(jnp.mean(x32 * x32, axis=-1, keepdims=True) + EPS)
    return (y * w.astype(jnp.float32)).astype(x.dtype)


def l2_norm(x):
    return x * lax.rsqrt(jnp.sum(x * x, axis=-1, keepdims=True) + EPS)


def split_cols(a, sizes):
    out, start = [], 0
    for s in sizes:
        out.append(a[..., start:start + s])
        start += s
    return out


def heads(a, n_heads):
    b, t, _ = a.shape
    return jnp.moveaxis(a.reshape(b, t, n_heads, -1), 2, 1)


def to_chunks(a, chunk):
    b, h, t = a.shape[:3]
    return jnp.moveaxis(a.reshape(b, h, t // chunk, chunk, *a.shape[3:]), 2, 0)


def from_chunks(a):
    n, b, h, c, d = a.shape
    return jnp.moveaxis(a, 0, 2).reshape(b, h, n * c, d)


def masked_exp(mask, diff):
    return jnp.where(mask, jnp.exp(jnp.where(mask, diff, 0.0)), 0.0)


def causal_dwconv(x, buf, w):
    width = w.shape[0]
    t = x.shape[1]
    xp = jnp.concatenate([buf.astype(x.dtype), x], axis=1)
    wx = w.astype(x.dtype)
    y = xp[:, 0:t] * wx[0]
    for j in range(1, width):
        y = y + xp[:, j:j + t] * wx[j]
    return y, xp[:, -(width - 1):]


def gated_head_norm(o, gate, w):
    b, h, t, d = o.shape
    o = o * lax.rsqrt(jnp.mean(o * o, axis=-1, keepdims=True) + EPS) * w.astype(jnp.float32)
    o = jnp.moveaxis(o, 1, 2).reshape(b, t, h * d)
    return (o * jax.nn.silu(gate.astype(jnp.float32))).astype(gate.dtype)


def gated_linear_scan(q, k, v, log_f, state, chunk):
    causal = jnp.tril(jnp.ones((chunk, chunk), dtype=bool))[:, :, None]

    def step(s, inp):
        qc, kc, vc, gc = inp
        b = jnp.cumsum(gc, axis=2)
        decay = masked_exp(causal, b[:, :, :, None, :] - b[:, :, None, :, :])
        attn = jnp.einsum('bhtd,bhsd,bhtsd->bhts', qc, kc, decay)
        o = (jnp.einsum('bhtd,bhde->bhte', qc * jnp.exp(b), s)
             + jnp.einsum('bhts,bhse->bhte', attn, vc))
        b_last = b[:, :, -1, :]
        s = (jnp.exp(b_last)[..., None] * s
             + jnp.einsum('bhsd,bhse->bhde', kc * jnp.exp(b_last[:, :, None, :] - b), vc))
        return s, o

    s, o = lax.scan(step, state, (to_chunks(q, chunk), to_chunks(k, chunk),
                                  to_chunks(v, chunk), to_chunks(log_f, chunk)))
    return from_chunks(o), s


def gated_delta_scan(q, k, v, beta, log_a, state, chunk):
    dv = v.shape[-1]
    causal = jnp.tril(jnp.ones((chunk, chunk), dtype=bool))
    eye = jnp.eye(chunk, dtype=bool)
    strict = causal & ~eye
    eye_f = eye.astype(jnp.float32)

    def step(s, inp):
        qc, kc, vc, bc, gc = inp
        b = jnp.cumsum(gc, axis=-1)
        decay = masked_exp(causal, b[..., :, None] - b[..., None, :])
        kk = jnp.einsum('bhtd,bhsd->bhts', kc, kc)
        a_mat = jnp.where(strict, bc[..., :, None] * kk * decay, 0.0) + eye_f
        rhs = jnp.concatenate([vc * bc[..., None], kc * (bc * jnp.exp(b))[..., None]], axis=-1)
        sol = lax.linalg.triangular_solve(a_mat, rhs, left_side=True, lower=True,
                                          unit_diagonal=True)
        u, w = sol[..., :dv], sol[..., dv:]
        v_new = u - jnp.einsum('bhtd,bhde->bhte', w, s)
        qk = jnp.einsum('bhtd,bhsd->bhts', qc, kc) * decay
        o = (jnp.einsum('bhtd,bhde->bhte', qc * jnp.exp(b)[..., None], s)
             + jnp.einsum('bhts,bhse->bhte', qk, v_new))
        b_last = b[..., -1]
        s = (jnp.exp(b_last)[..., None, None] * s
             + jnp.einsum('bhsd,bhse->bhde', kc * jnp.exp(b_last[..., None] - b)[..., None], v_new))
        return s, o

    s, o = lax.scan(step, state, (to_chunks(q, chunk), to_chunks(k, chunk), to_chunks(v, chunk),
                                  to_chunks(beta, chunk), to_chunks(log_a, chunk)))
    return from_chunks(o), s


def gla_mixer(q, k, v, g, lr, w_gate, b_gate, norm_w, state, chunk):
    f32 = jnp.float32
    log_f = jax.nn.log_sigmoid((lr @ w_gate + b_gate).astype(f32)) / GLA_GATE_NORM
    qh = heads(q.astype(f32), GLA_H) * (GLA_DK ** -0.5)
    o, s = gated_linear_scan(qh, heads(k.astype(f32), GLA_H), heads(v.astype(f32), GLA_H),
                             heads(log_f, GLA_H), state.astype(f32), chunk)
    return gated_head_norm(o, g, norm_w), s


def gdn_mixer(qkv, z, b_raw, a_raw, conv_w, a_log, dt_bias, norm_w, state, conv_buf, chunk):
    f32 = jnp.float32
    qkv, new_buf = causal_dwconv(qkv, conv_buf, conv_w)
    qkv = jax.nn.silu(qkv.astype(f32))
    q, k, v = split_cols(qkv, (GDN_KW, GDN_KW, GDN_VW))
    q = l2_norm(heads(q, GDN_H)) * (GDN_DK ** -0.5)
    k = l2_norm(heads(k, GDN_H))
    v = heads(v, GDN_H)
    beta = jnp.moveaxis(jax.nn.sigmoid(b_raw.astype(f32)), 2, 1)
    log_a = jnp.moveaxis(-jnp.exp(a_log.astype(f32))
                         * jax.nn.softplus(a_raw.astype(f32) + dt_bias.astype(f32)), 2, 1)
    o, s = gated_delta_scan(q, k, v, beta, log_a, state.astype(f32), chunk)
    return gated_head_norm(o, z, norm_w), s, new_buf


def hgrn_mixer(q, f, i, g, lb, norm_w, state, chunk):
    f32 = jnp.float32
    zf = f.astype(f32)
    log_lb = jnp.log(jnp.maximum(lb, TINY))
    log_f = jnp.logaddexp(log_lb, jnp.log1p(-lb) + jax.nn.log_sigmoid(zf))
    key = (1.0 - lb) * jax.nn.sigmoid(-zf)
    qh = heads(jax.nn.silu(q.astype(f32)), HG_H)
    o, s = gated_linear_scan(qh, heads(key, HG_H), heads(i.astype(f32), HG_H),
                             heads(log_f, HG_H), state.astype(f32), chunk)
    return gated_head_norm(o, g, norm_w), s


def trunk(x, pe, st_gla, st_gdn, cb_gdn, st_hg, cb_ffn, params, chunk):
    (norm_mix, w_in, w_gla_gate, b_gla_gate, gla_norm, w_gdn_conv, gdn_a_log, gdn_dt_bias,
     gdn_norm, hgrn_lb, hgrn_norm, w_out, norm_ffn, w_up, w_ffn_conv, w_down, norm_ple,
     w_ple_gate, w_ple_proj, norm_final) = params
    sm = jax.nn.softmax(hgrn_lb.astype(jnp.float32), axis=0)
    lower_bounds = jnp.cumsum(sm, axis=0) - sm[0]
    h = x
    n_gla, n_gdn, n_gconv, n_hg, n_fconv = [], [], [], [], []
    for li in range(DEPTH):
        xn = rms_norm(h, norm_mix[li])
        (gq, gk, gv, gg, glr, dqkv, dz, db, da, hq, hf, hi, hg) = split_cols(xn @ w_in[li], IN_SPLITS)
        o_gla, s_gla = gla_mixer(gq, gk, gv, gg, glr, w_gla_gate[li], b_gla_gate[li],
                                 gla_norm[li], st_gla[li], chunk)
        o_gdn, s_gdn, b_gdn = gdn_mixer(dqkv, dz, db, da, w_gdn_conv[li], gdn_a_log[li],
                                        gdn_dt_bias[li], gdn_norm[li], st_gdn[li], cb_gdn[li], chunk)
        o_hg, s_hg = hgrn_mixer(hq, hf, hi, hg, lower_bounds[li], hgrn_norm[li], st_hg[li], chunk)
        h = h + jnp.concatenate([o_gla, o_gdn, o_hg], axis=-1) @ w_out[li]
        xn = rms_norm(h, norm_ffn[li])
        up, b_ffn = causal_dwconv(xn @ w_up[li], cb_ffn[li], w_ffn_conv[li])
        gate, val = split_cols(up, (D_FF, D_FF))
        h = h + (jax.nn.silu(gate) * val) @ w_down[li]
        ple_gate = jax.nn.sigmoid(rms_norm(h, norm_ple[li]) @ w_ple_gate[li])
        h = h + ple_gate * (pe[li].astype(h.dtype) @ w_ple_proj[li])
        n_gla.append(s_gla)
        n_gdn.append(s_gdn)
        n_gconv.append(b_gdn)
        n_hg.append(s_hg)
        n_fconv.append(b_ffn)
    y = rms_norm(h, norm_final)
    return (y, jnp.stack(n_gla), jnp.stack(n_gdn), jnp.stack(n_gconv),
            jnp.stack(n_hg), jnp.stack(n_fconv))


def setup_inputs(seed: int = 0) -> dict:
    key = jax.random.key(seed)
    keys = jax.random.split(key, 32)
    counter = [0]

    def nxt():
        k = keys[counter[0]]
        counter[0] += 1
        return k

    def nrm(shape, scale):
        return jax.random.normal(nxt(), shape, jnp.float32) * scale

    def gain(shape):
        return 1.0 + nrm(shape, 0.05)

    return {
        'x_prompt': nrm((BATCH, SEQ, D_MODEL), 1.0),
        'x_sample': nrm((DEC_BATCH, DEC_SEQ, D_MODEL), 1.0),
        'p_prompt': nrm((DEPTH, BATCH, SEQ, D_PLE), 1.0),
        'p_sample': nrm((DEPTH, DEC_BATCH, DEC_SEQ, D_PLE), 1.0),
        'state_gla': nrm((DEPTH, DEC_BATCH, GLA_H, GLA_DK, GLA_DV), 0.5),
        'state_gdn': nrm((DEPTH, DEC_BATCH, GDN_H, GDN_DK, GDN_DV), 0.1),
        'cache_gdn_conv': nrm((DEPTH, DEC_BATCH, GDN_CONV - 1, GDN_QKV), 1.0),
        'state_hgrn': nrm((DEPTH, DEC_BATCH, HG_H, HG_DK, HG_DV), 0.5),
        'cache_ffn_conv': nrm((DEPTH, DEC_BATCH, FFN_CONV - 1, 2 * D_FF), 1.0),
        'norm_mix': gain((DEPTH, D_MODEL)),
        'w_in': nrm((DEPTH, D_MODEL, N_IN), D_MODEL ** -0.5),
        'w_gla_gate': nrm((DEPTH, GLA_RANK, GLA_KW), GLA_RANK ** -0.5),
        'b_gla_gate': nrm((DEPTH, GLA_KW), 0.1),
        'gla_norm': gain((DEPTH, GLA_DV)),
        'w_gdn_conv': nrm((DEPTH, GDN_CONV, GDN_QKV), GDN_CONV ** -0.5),
        'gdn_a_log': jnp.log(jax.random.uniform(nxt(), (DEPTH, GDN_H), jnp.float32, 1.0, 16.0)),
        'gdn_dt_bias': nrm((DEPTH, GDN_H), 0.1),
        'gdn_norm': gain((DEPTH, GDN_DV)),
        'hgrn_lb': nrm((DEPTH, HG_KW), 1.0),
        'hgrn_norm': gain((DEPTH, HG_DV)),
        'w_out': nrm((DEPTH, D_MIX, D_MODEL), D_MIX ** -0.5),
        'norm_ffn': gain((DEPTH, D_MODEL)),
        'w_up': nrm((DEPTH, D_MODEL, 2 * D_FF), D_MODEL ** -0.5),
        'w_ffn_conv': nrm((DEPTH, FFN_CONV, 2 * D_FF), FFN_CONV ** -0.5),
        'w_down': nrm((DEPTH, D_FF, D_MODEL), D_FF ** -0.5),
        'norm_ple': gain((DEPTH, D_MODEL)),
        'w_ple_gate': nrm((DEPTH, D_MODEL, D_MODEL), D_MODEL ** -0.5),
        'w_ple_proj': nrm((DEPTH, D_PLE, D_MODEL), D_PLE ** -0.5),
        'norm_final': gain((D_MODEL,)),
    }


def reference(x_prompt, x_sample, p_prompt, p_sample, state_gla, state_gdn, cache_gdn_conv,
              state_hgrn, cache_ffn_conv, norm_mix, w_in, w_gla_gate, b_gla_gate, gla_norm,
              w_gdn_conv, gdn_a_log, gdn_dt_bias, gdn_norm, hgrn_lb, hgrn_norm, w_out, norm_ffn,
              w_up, w_ffn_conv, w_down, norm_ple, w_ple_gate, w_ple_proj, norm_final):
    params = (norm_mix, w_in, w_gla_gate, b_gla_gate, gla_norm, w_gdn_conv, gdn_a_log,
              gdn_dt_bias, gdn_norm, hgrn_lb, hgrn_norm, w_out, norm_ffn, w_up, w_ffn_conv,
              w_down, norm_ple, w_ple_gate, w_ple_proj, norm_final)
    f32 = jnp.float32
    bp, tp = x_prompt.shape[0], x_prompt.shape[1]
    zero_gla = jnp.zeros((DEPTH, bp, GLA_H, GLA_DK, GLA_DV), f32)
    zero_gdn = jnp.zeros((DEPTH, bp, GDN_H, GDN_DK, GDN_DV), f32)
    zero_gconv = jnp.zeros((DEPTH, bp, GDN_CONV - 1, GDN_QKV), x_prompt.dtype)
    zero_hg = jnp.zeros((DEPTH, bp, HG_H, HG_DK, HG_DV), f32)
    zero_fconv = jnp.zeros((DEPTH, bp, FFN_CONV - 1, 2 * D_FF), x_prompt.dtype)
    (y_prompt, p_state_gla, p_state_gdn, p_cache_gdn_conv, p_state_hgrn,
     p_cache_ffn_conv) = trunk(x_prompt, p_prompt, zero_gla, zero_gdn, zero_gconv, zero_hg,
                               zero_fconv, params, min(CHUNK, tp))
    (y_sample, s_state_gla, s_state_gdn, s_cache_gdn_conv, s_state_hgrn,
     s_cache_ffn_conv) = trunk(x_sample, p_sample, state_gla, state_gdn, cache_gdn_conv,
                               state_hgrn, cache_ffn_conv, params, x_sample.shape[1])
    return (y_prompt, y_sample, p_state_gla, p_state_gdn, p_cache_gdn_conv, p_state_hgrn,
            p_cache_ffn_conv, s_state_gla, s_state_gdn, s_cache_gdn_conv, s_state_hgrn,
            s_cache_ffn_conv)
```

```python
import numpy as np
import concourse.bass as bass
import concourse.mybir as mybir
from concourse.bass_utils import run_bass_kernel_spmd

F32 = mybir.dt.float32
BF16 = mybir.dt.bfloat16
AF = mybir.ActivationFunctionType
ALU = mybir.AluOpType

D = 4096
KC = 32
T = 416
C = 32
NCH = 13
DEPTH = 2
D_FF = 11008
NJ = 86
N_IN = 15408
EPS = 1e-6
SEQ = 2048
NCOL = SEQ + 32
SLOT = 8192
NSLOT = 3
NCONST = 1792

TILES_FULL = [(0, 13, False), (416, 13, False), (832, 13, False), (1248, 13, False), (1664, 12, True)]
CFG = {"tiles": TILES_FULL, "mixers": True, "ffn": True, "ple": True}

O_GQ, O_GK, O_GV, O_GG, O_GLR = 0, 512, 1024, 2048, 3072
O_DQ, O_DK, O_DV, O_DZ, O_DB, O_DA = 3088, 5136, 7184, 9232, 11280, 11296
O_HQ, O_HF, O_HI, O_HG = 11312, 12336, 13360, 14384


class Tok:
    __slots__ = ("sem", "key", "val")

    def __init__(self, sem, key, val):
        self.sem = sem
        self.key = key
        self.val = val


class Buf:
    __slots__ = ("name", "w", "r")

    def __init__(self, name):
        self.name = name
        self.w = None
        self.r = {}


class Eng:
    def __init__(self, nc, e, name, key):
        self.e = e
        self.name = name
        self.key = key
        self.sem = nc.alloc_semaphore("sem_" + name)
        self.count = 0
        self.seen = {}


class KB:
    def __init__(self, nc):
        self.nc = nc
        self.E = {
            "pe": Eng(nc, nc.tensor, "pe", 0),
            "act": Eng(nc, nc.scalar, "act", 1),
            "dve": Eng(nc, nc.vector, "dve", 2),
            "pool": Eng(nc, nc.gpsimd, "pool", 3),
            "sp": Eng(nc, nc.sync, "sp", 4),
        }
        self.nds = 20
        self.dsem = [nc.alloc_semaphore(f"dsem{i}") for i in range(self.nds)]
        self.dcount = 0
        self.wsem = [nc.alloc_semaphore(f"wsem{i}") for i in range(NSLOT)]
        self.wcount = [0] * NSLOT
        self.out_toks = []
        self.all_dma = []

    def _wait(self, E, toks):
        for t in toks:
            if t is None:
                continue
            if E.seen.get(t.key, 0) >= t.val:
                continue
            E.e.wait_ge(t.sem, t.val)
            E.seen[t.key] = t.val

    def _deps(self, E, reads, writes):
        deps = []
        for b in reads:
            if b.w is not None:
                deps.append(b.w)
        for b in writes:
            if b.w is not None and b.w.key != E.key:
                deps.append(b.w)
            for t in b.r.values():
                if t.key != E.key:
                    deps.append(t)
        return deps

    def _record(self, tok, reads, writes):
        for b in reads:
            o = b.r.get(tok.key)
            if o is None or o.val < tok.val:
                b.r[tok.key] = tok
        for b in writes:
            b.w = tok
            b.r = {}

    def op(self, eng, fn, reads=(), writes=(), inc=True):
        E = self.E[eng]
        self._wait(E, self._deps(E, reads, writes))
        ins = fn(E.e)
        if inc:
            E.count += 1
            ins.then_inc(E.sem, 1)
            tok = Tok(E.sem, E.key, E.count)
        else:
            tok = Tok(E.sem, E.key, E.count + 1)
        self._record(tok, reads, writes)
        return tok

    def dma(self, out, in_, reads=(), writes=(), q="sp", is_output=False):
        E = self.E[q]
        n = self.dcount
        self.dcount += 1
        j = n % self.nds
        sem = self.dsem[j]
        key = 100 + j
        prev = n // self.nds
        deps = self._deps(E, reads, writes)
        if prev > 0:
            deps.append(Tok(sem, key, 16 * prev))
        self._wait(E, deps)
        E.e.dma_start(out=out, in_=in_).then_inc(sem, 16)
        tok = Tok(sem, key, 16 * (prev + 1))
        self._record(tok, reads, writes)
        self.all_dma.append(tok)
        if is_output:
            self.out_toks.append(tok)
        return tok

    def finish(self):
        E = self.E["sp"]
        self._wait(E, self.out_toks)


class Builder:
    def __init__(self, nc, cfg):
        self.nc = nc
        self.cfg = cfg
        self.kb = KB(nc)
        self.wnext = 0
        self.nbig = 0
        self.nmisc = 0
        self._uid = 0

    def sb(self, name, shape, dt):
        t = self.nc.alloc_sbuf_tensor(name, list(shape), dt)
        return t.ap()

    def buf(self, name="b"):
        self._uid += 1
        return Buf(f"{name}{self._uid}")

    def declare_io(self):
        nc = self.nc

        def din(name, shape):
            return nc.dram_tensor(name, list(shape), F32, kind="ExternalInput").ap()

        def dout(name, shape):
            return nc.dram_tensor(name, list(shape), F32, kind="ExternalOutput").ap()

        self.xin = din("xin", [D, NCOL])
        self.pin = din("pin", [DEPTH, 256, NCOL])
        self.st_gla = din("st_gla", [DEPTH, 4, 128, 256])
        self.st_gdn = din("st_gdn", [DEPTH, 16, 128, 128])
        self.st_hg = din("st_hg", [DEPTH, 8, 128, 128])
        self.cg_in = din("cg_in", [DEPTH, 128, 48 * 3])
        self.cf_in = din("cf_in", [DEPTH, 128, 172 * 2])
        self.w_in = din("w_in", [DEPTH, D, N_IN])
        self.w_out = din("w_out", [DEPTH, D, D])
        self.w_up = din("w_up", [DEPTH, D, 2 * D_FF])
        self.w_down = din("w_down", [DEPTH, D_FF, D])
        self.w_pg = din("w_pg", [DEPTH, D, D])
        self.w_pp = din("w_pp", [DEPTH, 256, D])
        self.w_gg = din("w_gg", [DEPTH, 16, 512])
        self.vecs = din("vecs", [128, 7 * 32])
        self.cw_ffn = din("cw_ffn", [DEPTH, 128, 172 * 3])
        self.cw_gdn = din("cw_gdn", [DEPTH, 128, 48 * 4])
        self.smallp = din("smallp", [128, 64])
        self.hlb = din("hlb", [128, 16])
        self.gdnrow = din("gdnrow", [128, 832])
        self.consts = din("consts", [128, NCONST])
        self.hspill = nc.dram_tensor("hspill", [128, KC * T], F32, kind="Internal").ap()
        self.st_scr = nc.dram_tensor("st_scr", [DEPTH, 128, 4096], F32, kind="Internal").ap()
        self.dbg = dout("dbg", [16, 128, T]) if self.cfg.get("dbg") else None
        self.yout = dout("yout", [D, NCOL])
        self.o_st_gla = dout("o_st_gla", [2, DEPTH, 4, 128, 256])
        self.o_st_gdn = dout("o_st_gdn", [2, DEPTH, 16, 128, 128])
        self.o_st_hg = dout("o_st_hg", [2, DEPTH, 8, 128, 128])
        self.o_cg = dout("o_cg", [2, DEPTH, 128, 48 * 3])
        self.o_cf = dout("o_cf", [2, DEPTH, 128, 172 * 2])

    def alloc(self):
        nc = self.nc
        self.h = self.sb("h", [128, KC, T], F32)
        self.hflat = self.h.rearrange("p k t -> p (k t)")
        self.hb = [self.buf("h") for _ in range(KC)]
        self.xn = self.sb("xn", [128, KC, T], BF16)
        self.xnb = [self.buf("xn") for _ in range(KC)]
        self.ARENA = 21504
        self.arena = self.sb("arena", [128, self.ARENA], BF16)
        self.mix = self.arena[:, 0:KC * T].rearrange("p (k t) -> p k t", k=KC)
        self.mixb = [self.buf("mix") for _ in range(KC)]
        self.act = self.arena[:, 0:43 * T].rearrange("p (k t) -> p k t", k=43)
        self.actb = [self.buf("act") for _ in range(43)]
        self.ring = [self.sb(f"ring{i}", [128, SLOT], BF16) for i in range(NSLOT)]
        self.ringb = [self.buf("ring") for _ in range(NSLOT)]
        self.pbank = [nc.alloc_psum_tensor(f"pb{i}", [128, 512], F32).ap() for i in range(8)]
        self.pbb = [self.buf("pb") for _ in range(8)]
        self.vecs_sb = self.sb("vecs_sb", [128, 7 * 32], F32)
        self.vecs_b = self.buf("vecs")
        self.consts_sb = self.sb("consts_sb", [128, NCONST], F32)
        self.consts_b = self.buf("consts")
        self.ones_bf = self.sb("ones_bf", [128, 128], BF16)
        self.ident_bf = self.sb("ident_bf", [128, 128], BF16)
        self.cbf_b = self.buf("cbf")
        self.sq = [self.sb(f"sq{i}", [128, T], BF16) for i in range(2)]
        self.sqb = [self.buf("sq") for _ in range(2)]
        self.rstd = self.sb("rstd", [128, T], F32)
        self.rstd_b = self.buf("rstd")
        self.cwf = self.sb("cwf", [128, DEPTH, 172 * 3], F32)
        self.cwf_b = self.buf("cwf")
        self.fh = self.sb("fh", [128, DEPTH, 172 * 2], F32)
        self.fhb = [self.buf("fh") for _ in range(DEPTH)]
        self.fhs = self.sb("fhs", [128, DEPTH, 172 * 2], F32)
        self.fhsb = [self.buf("fhs") for _ in range(DEPTH)]
        self.NSCR = 4
        self.scr = [self.sb(f"scr{i}", [128, T + 4], F32) for i in range(self.NSCR)]
        self.scrb = [self.buf("scr") for _ in range(self.NSCR)]
        self.nscr = 0
        self.NLNG = 5
        self.lng = [self.sb(f"lng{i}", [128, T], F32) for i in range(self.NLNG)]
        self.lngb = [self.buf("lng") for _ in range(self.NLNG)]
        self.nlng = 0
        self.pe_bf = self.arena[:, 0:2 * T].rearrange("p (k t) -> p k t", k=2)
        self.pebf_b = self.buf("pe_bf")
        self.cwg = self.sb("cwg", [128, DEPTH, 48 * 4], F32)
        self.cwg_b = self.buf("cwg")
        self.gh = self.sb("gh", [128, DEPTH, 48 * 3], F32)
        self.ghb = [self.buf("gh") for _ in range(DEPTH)]
        self.ghs = self.sb("ghs", [128, DEPTH, 48 * 3], F32)
        self.ghsb = [self.buf("ghs") for _ in range(DEPTH)]
        self.grow_b = self.buf("grow")
        self.slotb = [self.buf("slot") for _ in range(3)]

    def scratch(self):
        i = self.nscr % self.NSCR
        self.nscr += 1
        return self.scr[i], self.scrb[i]

    def scratch_long(self):
        i = self.nlng % self.NLNG
        self.nlng += 1
        return self.lng[i], self.lngb[i]

    def big_bank(self):
        i = self.nbig % 4
        self.nbig += 1
        return self.pbank[i], self.pbb[i]

    def misc_bank(self):
        i = 4 + self.nmisc % 4
        self.nmisc += 1
        return self.pbank[i], self.pbb[i]

    def wload(self, src, nk, ncols):
        kb = self.kb
        i = self.wnext % NSLOT
        self.wnext += 1
        assert nk * ncols <= SLOT
        dst = self.ring[i][:, 0:nk * ncols].rearrange("p (k n) -> p k n", k=nk)
        b = self.ringb[i]
        E = kb.E["pool"]
        kb._wait(E, kb._deps(E, (), (b,)))
        sem = kb.wsem[i]
        kb.wcount[i] += 1
        if nk == 1:
            E.e.dma_start(out=self.ring[i][:, 0:ncols], in_=src).then_inc(sem, 16)
        else:
            E.e.dma_start(out=dst, in_=src.rearrange("(k p) n -> p k n", p=128)).then_inc(sem, 16)
        tok = Tok(sem, 200 + i, 16 * kb.wcount[i])
        kb._record(tok, (), (b,))
        return dst, b

    def mm(self, out, lhsT, rhs, start, stop, reads, writes, inc):
        return self.kb.op("pe", lambda e: e.matmul(out, lhsT=lhsT, rhs=rhs, start=start, stop=stop),
                          reads=reads, writes=writes, inc=inc)

    def proj_fm(self, W, c0, ncols, rhs, rhsb, handler, k0=0, nk=KC, ucols=256):
        ucols = min(ucols, SLOT // nk)
        col = 0
        ci = 0
        while col < ncols:
            uc = min(ucols, ncols - col)
            wv, wb = self.wload(W[k0 * 128:(k0 + nk) * 128, c0 + col:c0 + col + uc], nk, uc)
            m0 = 0
            while m0 < uc:
                mw = min(128, uc - m0)
                bank, bb = self.big_bank()
                for k in range(nk):
                    self.mm(bank[0:mw, 0:T], wv[:, k, m0:m0 + mw], rhs[k], k == 0, k == nk - 1,
                            reads=(wb, rhsb[k]), writes=(bb,), inc=(k == nk - 1))
                handler(ci, mw, bank, bb)
                ci += 1
                m0 += mw
            col += uc

    def norm(self, wcol0, out_fn):
        kb = self.kb
        bank, bb = self.misc_bank()
        for c in range(KC):
            s = c % 2
            kb.op("act", lambda e, c=c, s=s: e.activation(out=self.sq[s][:], in_=self.h[:, c, :], func=AF.Square),
                  reads=(self.hb[c],), writes=(self.sqb[s],))
            self.mm(bank[:, 0:T], self.ones_bf[:], self.sq[s][:], c == 0, c == KC - 1,
                    reads=(self.sqb[s], self.cbf_b), writes=(bb,), inc=True)
        kb.op("act", lambda e: e.activation(out=self.rstd[:], in_=bank[:, 0:T], func=AF.Sqrt,
                                            scale=1.0 / D, bias=self.eps_col),
              reads=(bb, self.consts_b), writes=(self.rstd_b,))
        kb.op("dve", lambda e: e.reciprocal(out=self.rstd[:], in_=self.rstd[:]),
              reads=(self.rstd_b,), writes=(self.rstd_b,))
        for c in range(KC):
            out_fn(c, self.vecs_sb[:, wcol0 + c:wcol0 + c + 1])

    def norm_to_xn(self, wcol0):
        kb = self.kb

        def f(c, wcol):
            kb.op("dve", lambda e: e.scalar_tensor_tensor(out=self.xn[:, c, :], in0=self.h[:, c, :], scalar=wcol,
                                                          in1=self.rstd[:], op0=ALU.mult, op1=ALU.mult),
                  reads=(self.hb[c], self.rstd_b, self.vecs_b), writes=(self.xnb[c],))
        self.norm(wcol0, f)

    def segs(self, tile):
        p0, npc, hs = tile
        s = [("p", 0, npc * C)]
        if hs:
            s.append(("s", npc * C, C))
        return s

    def conv_chunk(self, tile, bank, bb, W, wtile, wbuf, wbase, hist_p, hist_pb, hist_s, hist_sb, hbase, first_tile):
        kb = self.kb
        H = W - 1
        cy, cyb = self.scratch_long()
        for (kind, col0, n) in self.segs(tile):
            st, stb = self.scratch()
            hist, histb = (hist_p, hist_pb) if kind == "p" else (hist_s, hist_sb)
            hv = hist[:, hbase:hbase + H]
            if kind == "p" and first_tile:
                kb.op("dve", lambda e, st=st: e.memset(st[:, 0:H], 0.0), reads=(), writes=(stb,))
            else:
                kb.op("dve", lambda e, st=st, hv=hv: e.tensor_copy(out=st[:, 0:H], in_=hv),
                      reads=(histb,), writes=(stb,))
            kb.op("act", lambda e, st=st, col0=col0, n=n: e.activation(out=st[:, H:H + n], in_=bank[:, col0:col0 + n],
                                                                    func=AF.Copy),
                  reads=(bb, stb), writes=(stb,))
            kb.op("dve", lambda e, st=st, hv=hv, n=n: e.tensor_copy(out=hv, in_=st[:, n:n + H]),
                  reads=(stb,), writes=(histb,))
            for j in range(W - 1, -1, -1):
                wc = wtile[:, wbase + j:wbase + j + 1]
                if j == W - 1:
                    kb.op("dve", lambda e, st=st, wc=wc, j=j, n=n, col0=col0: e.tensor_scalar(
                        out=cy[:, col0:col0 + n], in0=st[:, j:j + n], scalar1=wc, scalar2=None, op0=ALU.mult),
                        reads=(stb, wbuf), writes=(cyb,))
                else:
                    kb.op("dve", lambda e, st=st, wc=wc, j=j, n=n, col0=col0: e.scalar_tensor_tensor(
                        out=cy[:, col0:col0 + n], in0=st[:, j:j + n], scalar=wc, in1=cy[:, col0:col0 + n],
                        op0=ALU.mult, op1=ALU.add),
                        reads=(stb, wbuf, cyb), writes=(cyb,))
        return cy, cyb

    def ffn(self, li, tile, first_tile):
        kb = self.kb
        Wu = self.w_up[li]
        Wd = self.w_down[li]
        xn_list = [self.xn[:, c, :] for c in range(KC)]
        for half in range(2):
            j0 = half * 43
            for jp in range(0, 43, 2):
                nj = min(2, 43 - jp)
                res = {}

                def hnd(which, jp=jp):
                    def handler(ci, mw, bank, bb):
                        j = j0 + jp + ci
                        chunk = j if which == "g" else NJ + j
                        cy, cyb = self.conv_chunk(tile, bank, bb, 3, self.cwf[:, li, :], self.cwf_b, chunk * 3,
                                                  self.fh[:, li, :], self.fhb[li], self.fhs[:, li, :], self.fhsb[li],
                                                  chunk * 2, first_tile)
                        res[(which, ci)] = (cy, cyb)
                    return handler
                self.proj_fm(Wu, (j0 + jp) * 128, nj * 128, xn_list, self.xnb, hnd("g"))
                self.proj_fm(Wu, D_FF + (j0 + jp) * 128, nj * 128, xn_list, self.xnb, hnd("v"))
                for ci in range(nj):
                    jj = jp + ci
                    g, gb = res[("g", ci)]
                    v, vb = res[("v", ci)]
                    kb.op("act", lambda e, g=g: e.activation(out=g[:, 0:T], in_=g[:, 0:T], func=AF.Silu), reads=(gb,), writes=(gb,))
                    kb.op("dve", lambda e, g=g, v=v, jj=jj: e.tensor_tensor(out=self.act[:, jj, :], in0=g[:, 0:T], in1=v[:, 0:T], op=ALU.mult),
                          reads=(gb, vb), writes=(self.actb[jj],))
            act_list = [self.act[:, jj, :] for jj in range(43)]
            for mp in range(16):
                banks = [self.big_bank() for _ in range(2)]
                for (ka, kn) in ((0, 22), (22, 21)):
                    wv, wb = self.wload(Wd[(j0 + ka) * 128:(j0 + ka + kn) * 128, mp * 256:(mp + 1) * 256], kn, 256)
                    for mi in range(2):
                        bank, bb = banks[mi]
                        for k in range(kn):
                            kk = ka + k
                            self.mm(bank[:, 0:T], wv[:, k, mi * 128:(mi + 1) * 128], act_list[kk], kk == 0, kk == 42,
                                    reads=(wb, self.actb[kk]), writes=(bb,), inc=(k == kn - 1))
                for mi in range(2):
                    m = mp * 2 + mi
                    bank, bb = banks[mi]
                    kb.op("dve", lambda e, m=m, bank=bank: e.tensor_tensor(out=self.h[:, m, :], in0=bank[:, 0:T],
                                                                          in1=self.h[:, m, :], op=ALU.add),
                          reads=(bb, self.hb[m]), writes=(self.hb[m],))

    def ple(self, li, tile):
        kb = self.kb
        p0, npc, hs = tile
        self.barrier()
        for k in range(2):
            pt, ptb = self.scratch()
            kb.dma(pt[:, 0:npc * C], self.pin[li, k * 128:(k + 1) * 128, p0:p0 + npc * C], reads=(), writes=(ptb,))
            if hs:
                kb.dma(pt[:, npc * C:T], self.pin[li, k * 128:(k + 1) * 128, SEQ:SEQ + C], reads=(), writes=(ptb,))
            kb.op("dve", lambda e, k=k, pt=pt: e.tensor_copy(out=self.pe_bf[:, k, :], in_=pt[:, 0:T]), reads=(ptb,), writes=(self.pebf_b,))
        self.norm_to_xn(4 * 32 + li * 32)
        xn_list = [self.xn[:, c, :] for c in range(KC)]
        state = {}

        def handler(ci, mw, bank, bb):
            m = ci
            sg, sgb = self.scratch_long()
            kb.op("act", lambda e: e.activation(out=sg[:, 0:T], in_=bank[:, 0:T], func=AF.Sigmoid), reads=(bb,), writes=(sgb,))
            if m % 2 == 0:
                state["w"] = self.wload(self.w_pp[li][:, m * 128:(m + 2) * 128], 2, 256)
            wpv, wpb = state["w"]
            mo = (m % 2) * 128
            bank2, bb2 = self.big_bank()
            for k in range(2):
                self.mm(bank2[:, 0:T], wpv[:, k, mo:mo + 128], self.pe_bf[:, k, :], k == 0, k == 1,
                        reads=(wpb, self.pebf_b), writes=(bb2,), inc=(k == 1))
            kb.op("dve", lambda e: e.tensor_tensor(out=sg[:, 0:T], in0=bank2[:, 0:T], in1=sg[:, 0:T], op=ALU.mult),
                  reads=(bb2, sgb), writes=(sgb,))
            kb.op("dve", lambda e: e.tensor_tensor(out=self.h[:, m, :], in0=sg[:, 0:T], in1=self.h[:, m, :], op=ALU.add),
                  reads=(sgb, self.hb[m]), writes=(self.hb[m],))
        self.proj_fm(self.w_pg[li], 0, D, xn_list, self.xnb, handler)

    def out_proj(self, li):
        kb = self.kb
        mix_list = [self.mix[:, c, :] for c in range(KC)]

        def handler(ci, mw, bank, bb):
            m = ci
            kb.op("dve", lambda e: e.tensor_tensor(out=self.h[:, m, :], in0=bank[:, 0:T], in1=self.h[:, m, :], op=ALU.add),
                  reads=(bb, self.hb[m]), writes=(self.hb[m],))
        self.proj_fm(self.w_out[li], 0, D, mix_list, self.mixb, handler)

    def build(self):
        nc = self.nc
        kb = self.kb
        cfg = self.cfg
        self.declare_io()
        self.alloc()
        kb.dma(self.vecs_sb[:], self.vecs[:, :], writes=(self.vecs_b,))
        kb.dma(self.consts_sb[:], self.consts[:, :], writes=(self.consts_b,))
        kb.dma(self.cwf[:], self.cw_ffn.rearrange("l p n -> p l n"), writes=(self.cwf_b,))
        kb.dma(self.fhs[:], self.cf_in.rearrange("l p n -> p l n"), writes=tuple(self.fhsb))
        kb.dma(self.cwg[:], self.cw_gdn.rearrange("l p n -> p l n"), writes=(self.cwg_b,))
        kb.dma(self.ghs[:], self.cg_in.rearrange("l p n -> p l n"), writes=tuple(self.ghsb))
        self.eps_col = self.consts_sb[:, 256:257]
        self.one_col = self.consts_sb[:, 257:258]
        kb.op("dve", lambda e: e.tensor_copy(out=self.ident_bf[:], in_=self.consts_sb[:, 0:128]),
              reads=(self.consts_b,), writes=(self.cbf_b,))
        kb.op("dve", lambda e: e.tensor_copy(out=self.ones_bf[:], in_=self.consts_sb[:, 128:256]),
              reads=(self.consts_b,), writes=(self.cbf_b,))
        tiles = cfg["tiles"]
        for ti, tile in enumerate(tiles):
            p0, npc, hs = tile
            first_tile = (p0 == 0)
            last_tile = (p0 + npc * C == SEQ)
            kb.dma(self.h[:, :, 0:npc * C], self.xin[:, p0:p0 + npc * C].rearrange("(k p) t -> p k t", p=128),
                   reads=(), writes=tuple(self.hb))
            if hs:
                kb.dma(self.h[:, :, npc * C:T], self.xin[:, SEQ:SEQ + C].rearrange("(k p) t -> p k t", p=128),
                       reads=(), writes=tuple(self.hb))
            for li in range(DEPTH):
                if cfg["mixers"]:
                    self.norm_to_xn(li * 32)
                    self.mixers(li, tile, first_tile, last_tile)
                    self.out_proj(li)
                if cfg["ffn"]:
                    self.norm_to_xn(2 * 32 + li * 32)
                    self.ffn(li, tile, first_tile)
                if cfg["ple"]:
                    self.ple(li, tile)

            def yfn(c, wcol):
                ys, ysb = self.scratch()
                kb.op("dve", lambda e: e.scalar_tensor_tensor(out=ys[:, 0:T], in0=self.h[:, c, :], scalar=wcol,
                                                              in1=self.rstd[:], op0=ALU.mult, op1=ALU.mult),
                      reads=(self.hb[c], self.rstd_b, self.vecs_b), writes=(ysb,))
                kb.dma(self.yout[c * 128:(c + 1) * 128, p0:p0 + npc * C], ys[:, 0:npc * C],
                       reads=(ysb,), is_output=True)
                if hs:
                    kb.dma(self.yout[c * 128:(c + 1) * 128, SEQ:SEQ + C], ys[:, npc * C:T],
                           reads=(ysb,), is_output=True)
            self.norm(6 * 32, yfn)
            if last_tile:
                for li in range(DEPTH):
                    kb.dma(self.o_cf[0, li], self.fh[:, li, :], reads=(self.fhb[li],), is_output=True)
                    kb.dma(self.o_cg[0, li], self.gh[:, li, :], reads=(self.ghb[li],), is_output=True)
            if hs:
                for li in range(DEPTH):
                    kb.dma(self.o_cf[1, li], self.fhs[:, li, :], reads=(self.fhsb[li],), is_output=True)
                    kb.dma(self.o_cg[1, li], self.ghs[:, li, :], reads=(self.ghsb[li],), is_output=True)
        kb.finish()

    def tf32(self, off, n):
        return self.hflat[:, off:off + n]

    def tbf(self, off, n_bf):
        return self.hflat[:, off:off + n_bf // 2].bitcast(BF16)

    def dump(self, i, ap, rows=128, cols=T):
        if self.dbg is None:
            return
        kb = self.kb
        tmp, tmpb = self.scratch_long()
        kb.op("dve", lambda e: e.tensor_copy(out=tmp[0:rows, 0:cols], in_=ap), reads=tuple(self._dump_reads), writes=(tmpb,))
        kb.dma(self.dbg[i, 0:rows, 0:cols], tmp[0:rows, 0:cols], reads=(tmpb,), is_output=True)

    def barrier(self):
        kb = self.kb
        toks = [Tok(kb.E[n].sem, kb.E[n].key, kb.E[n].count) for n in ("pe", "act", "dve") if kb.E[n].count > 0]
        toks += kb.all_dma
        kb.all_dma = []
        for n in ("pe", "act", "dve", "sp"):
            kb._wait(kb.E[n], [t for t in toks if t.key != kb.E[n].key])

    def mixers(self, li, tile, first_tile, last_tile):
        kb = self.kb
        p0, npc, hs = tile
        W = self.w_in[li]
        kb.dma(self.hspill[:, :], self.hflat, reads=tuple(self.hb), writes=())
        self.barrier()
        tb = self.buf("tmp")
        self.S_all = self.arena[:, KC * T:KC * T + 8192].bitcast(F32)
        Sb = self.buf("S")
        if first_tile:
            kb.op("dve", lambda e: e.memset(self.S_all, 0.0), writes=(Sb,))
        else:
            kb.dma(self.S_all, self.st_scr[li], writes=(Sb,))
        self.Sb = Sb
        sp = self.tf32(12800, 64)
        spb = self.buf("sp")
        kb.dma(sp, self.smallp[:, :], writes=(spb,))
        hl = self.tf32(12864, 16)
        kb.dma(hl, self.hlb[:, :], writes=(spb,))
        negb = self.tf32(12880, 8)
        kb.op("dve", lambda e: e.tensor_scalar(out=negb, in0=sp[:, 0:8], scalar1=-1.0, scalar2=None, op0=ALU.mult),
              reads=(spb,), writes=(spb,))
        e01 = self.tf32(12888, 16)
        kb.op("act", lambda e: e.activation(out=e01, in_=hl, func=AF.Exp), reads=(spb,), writes=(spb,))
        den = self.tf32(12904, 8)
        kb.op("dve", lambda e: e.tensor_tensor(out=den, in0=e01[:, 0:8], in1=e01[:, 8:16], op=ALU.add), reads=(spb,), writes=(spb,))
        kb.op("dve", lambda e: e.reciprocal(out=den, in_=den), reads=(spb,), writes=(spb,))
        sm = self.tf32(12912, 16)
        kb.op("dve", lambda e: e.tensor_tensor(out=sm[:, 0:8], in0=e01[:, 0:8], in1=den, op=ALU.mult), reads=(spb,), writes=(spb,))
        kb.op("dve", lambda e: e.tensor_tensor(out=sm[:, 8:16], in0=e01[:, 8:16], in1=den, op=ALU.mult), reads=(spb,), writes=(spb,))
        lbv = self.tf32(12928, 8)
        if li == 0:
            kb.op("dve", lambda e: e.tensor_tensor(out=lbv, in0=sm[:, 0:8], in1=sm[:, 0:8], op=ALU.subtract), reads=(spb,), writes=(spb,))
        else:
            kb.op("dve", lambda e: e.tensor_tensor(out=lbv, in0=sm[:, 0:8], in1=sm[:, 8:16], op=ALU.add), reads=(spb,), writes=(spb,))
            kb.op("dve", lambda e: e.tensor_tensor(out=lbv, in0=lbv, in1=sm[:, 0:8], op=ALU.subtract), reads=(spb,), writes=(spb,))
        oml = self.tf32(12936, 8)
        kb.op("dve", lambda e: e.tensor_scalar(out=oml, in0=lbv, scalar1=-1.0, scalar2=1.0, op0=ALU.mult, op1=ALU.add),
              reads=(spb,), writes=(spb,))
        noml = self.tf32(12944, 8)
        kb.op("dve", lambda e: e.tensor_scalar(out=noml, in0=oml, scalar1=-1.0, scalar2=None, op0=ALU.mult), reads=(spb,), writes=(spb,))
        self.sp, self.spb, self.negb, self.lbv, self.oml, self.noml = sp, spb, negb, lbv, oml, noml
        xn_list = [self.xn[:, c, :] for c in range(KC)]
        self.lr_bf = self.tbf(128, 416)
        self.lrb = self.buf("lr")
        self.wgg_bf = self.tbf(336, 512)
        wtmp = self.tf32(12288, 512)
        kb.dma(wtmp[0:16, :], self.w_gg[li], writes=(self.lrb,))
        kb.op("dve", lambda e: e.tensor_copy(out=self.wgg_bf[0:16, :], in_=wtmp[0:16, :]), reads=(self.lrb,), writes=(self.lrb,))

        def lr_h(ci, mw, bank, bb):
            kb.op("act", lambda e: e.activation(out=self.lr_bf[0:16, :], in_=bank[0:16, 0:T], func=AF.Copy),
                  reads=(bb,), writes=(self.lrb,))
        self.proj_fm(W, O_GLR, 16, xn_list, self.xnb, lr_h)
        for hd in range(4):
            self.gla_like("gla", li, hd, tile, first_tile, last_tile)
        for hd in range(8):
            self.gla_like("hg", li, hd, tile, first_tile, last_tile)
        self.barrier()
        if self.cfg.get("gdn", True):
            self.gdn(li, tile, first_tile, last_tile)
        else:
            for c in range(8, 24):
                kb.op("dve", lambda e, c=c: e.memset(self.mix[:, c, :], 0.0), writes=(self.mixb[c],))
        if not last_tile and not hs:
            kb.dma(self.st_scr[li], self.S_all, reads=(Sb,))
        self.barrier()
        E = kb.E["sp"]
        kb._wait(E, [Tok(kb.E[n].sem, kb.E[n].key, kb.E[n].count) for n in ("pe", "act", "dve")])
        kb.dma(self.hflat, self.hspill[:, :], writes=tuple(self.hb))


    def gdn(self, li, tile, first_tile, last_tile):
        kb = self.kb
        p0, npc, hs = tile
        nch = npc + (1 if hs else 0)
        W = self.w_in[li]
        xn_list = [self.xn[:, c, :] for c in range(KC)]
        C_ = self.consts_sb
        I32, ones32, onesw = C_[0:32, 0:32], C_[0:32, 128:160], C_[0:32, 128:256]
        Tri, SU, neg32 = C_[0:32, 1344:1376], C_[0:32, 1376:1408], C_[0:32, 1408:1440]
        M1, M2, M3 = C_[0:32, 1440:1504], C_[0:32, 1504:1568], C_[0:32, 1568:1632]
        I2, Tri2 = C_[0:32, 1632:1696], C_[0:32, 1696:1760]
        cb = self.consts_b
        NG = nch * 16
        raw = self.tf32(128, 416)[0:32, :]
        g_all = self.tf32(544, 208)[0:32, :]
        lb_all = self.tf32(752, 208)[0:32, :]
        beta = self.tf32(960, 208)[0:32, :]
        bcs = self.tf32(1168, 208)[0:32, :]
        beb = self.tf32(1376, 208)[0:32, :]
        dd = self.tf32(1584, 208)[0:32, :]
        tA = self.tf32(1792, 208)[0:32, :]
        expA = self.tf32(2000, 208)[0:32, :]
        gb = self.buf("gates")
        wv, wb = self.wload(W[:, O_DB:O_DB + 32], KC, 32)
        bank, bb = self.big_bank()
        for n in range(nch):
            for k in range(KC):
                self.mm(bank[0:32, n * 32:(n + 1) * 32], self.xn[:, k, n * 32:(n + 1) * 32], wv[:, k, :], k == 0, k == KC - 1,
                        reads=(wb, self.xnb[k]), writes=(bb,), inc=(k == KC - 1))
        kb.op("act", lambda e: e.activation(out=raw[:, 0:nch * 32], in_=bank[0:32, 0:nch * 32], func=AF.Copy), reads=(bb,), writes=(gb,))
        raw3 = raw[:, 0:nch * 32].rearrange("p (n c) -> p n c", n=nch)
        v3 = lambda a: a[:, 0:NG].rearrange("p (n h) -> p n h", n=nch)
        grow = self.tf32(12288, 416)[0:32, :]
        kb.dma(grow, self.gdnrow[0:32, li * 416:(li + 1) * 416], writes=(self.grow_b,))
        alog = grow[:, 0:NG]
        dtb = grow[:, 208:208 + NG]
        kb.op("act", lambda e: e.activation(out=v3(beta), in_=raw3[:, :, 0:16], func=AF.Sigmoid), reads=(gb,), writes=(gb,))
        kb.op("act", lambda e: e.activation(out=lb_all[:, 0:NG], in_=beta[:, 0:NG], func=AF.Ln), reads=(gb,), writes=(gb,))
        kb.op("dve", lambda e: e.tensor_tensor(out=v3(tA), in0=raw3[:, :, 16:32], in1=v3(dtb), op=ALU.add), reads=(gb, self.grow_b), writes=(gb,))
        kb.op("act", lambda e: e.activation(out=tA[:, 0:NG], in_=tA[:, 0:NG], func=AF.Exp), reads=(gb,), writes=(gb,))
        kb.op("act", lambda e: e.activation(out=tA[:, 0:NG], in_=tA[:, 0:NG], func=AF.Ln, bias=self.one_col[0:32, :]), reads=(gb, cb), writes=(gb,))
        kb.op("act", lambda e: e.activation(out=expA[:, 0:NG], in_=alog, func=AF.Exp), reads=(self.grow_b,), writes=(gb,))
        kb.op("dve", lambda e: e.scalar_tensor_tensor(out=g_all[:, 0:NG], in0=tA[:, 0:NG], scalar=-1.0, in1=expA[:, 0:NG],
                                                      op0=ALU.mult, op1=ALU.mult), reads=(gb,), writes=(gb,))
        b6, b6b = self.pbank[6], self.pbb[6]
        b7, b7b = self.pbank[7], self.pbb[7]
        self.mm(b6[0:32, 0:NG], Tri, g_all[:, 0:NG], True, True, reads=(cb, gb), writes=(b6b,), inc=True)
        kb.op("dve", lambda e: e.tensor_tensor(out=beb[:, 0:NG], in0=b6[0:32, 0:NG], in1=lb_all[:, 0:NG], op=ALU.add), reads=(b6b, gb), writes=(gb,))
        kb.op("act", lambda e: e.activation(out=beb[:, 0:NG], in_=beb[:, 0:NG], func=AF.Exp), reads=(gb,), writes=(gb,))
        self.mm(b7[0:32, 0:NG], SU, g_all[:, 0:NG], True, True, reads=(cb, gb), writes=(b7b,), inc=True)
        kb.op("act", lambda e: e.activation(out=dd[:, 0:NG], in_=b7[0:32, 0:NG], func=AF.Exp), reads=(b7b,), writes=(gb,))
        for grp in range(8):
            self.gdn_group(li, grp, tile, first_tile, last_tile, g_all, lb_all, beta, beb, dd, gb)

    def gdn_group(self, li, grp, tile, first_tile, last_tile, g_all, lb_all, beta, beb, dd, gb):
        kb = self.kb
        p0, npc, hs = tile
        nch = npc + (1 if hs else 0)
        W = self.w_in[li]
        xn_list = [self.xn[:, c, :] for c in range(KC)]
        C_ = self.consts_sb
        I32, ones32, onesw = C_[0:32, 0:32], C_[0:32, 128:160], C_[0:32, 128:256]
        Tri, SU, neg32 = C_[0:32, 1344:1376], C_[0:32, 1376:1408], C_[0:32, 1408:1440]
        M1, M2, M3 = C_[0:32, 1440:1504], C_[0:32, 1504:1568], C_[0:32, 1568:1632]
        I2, Tri2 = C_[0:32, 1632:1696], C_[0:32, 1696:1760]
        cb = self.consts_b
        h0 = grp * 2
        S_bf = self.tbf(0, 256)
        S_bfb = [self.buf("Sbf") for _ in range(2)]
        Sb = self.Sb
        qn = self.tbf(2208, 832).rearrange("p (h t) -> p h t", h=2)
        kn = self.tbf(2624, 832).rearrange("p (h t) -> p h t", h=2)
        vn = self.tbf(3040, 832).rearrange("p (h t) -> p h t", h=2)
        zg = self.tbf(3456, 832).rearrange("p (h t) -> p h t", h=2)
        qe = self.tbf(3872, 832).rearrange("p (h t) -> p h t", h=2)
        nwT = self.tbf(4288, 832).rearrange("p (h t) -> p h t", h=2)
        EB = self.tf32(4704, 832).rearrange("p (h t) -> p h t", h=2)
        k_tok = self.tbf(5536, 13 * 256)[0:32, :].rearrange("p (n h d) -> p n h d", n=13, h=2)
        v_tok = self.tbf(7200, 13 * 256)[0:32, :].rearrange("p (n h d) -> p n h d", n=13, h=2)
        Gtri = self.tf32(8864, 832)[0:32, :].rearrange("p (n h j) -> p n h j", n=13, h=2)
        PT = self.tbf(9696, 832)[0:32, :].rearrange("p (n h j) -> p n h j", n=13, h=2)
        TB = self.tbf(10112, 832)[0:32, :].rearrange("p (n h j) -> p n h j", n=13, h=2)
        TW = self.tbf(10528, 832)[0:32, :].rearrange("p (n h j) -> p n h j", n=13, h=2)
        vnw = self.tbf(12952, 512)[0:32, :].rearrange("p (a h d) -> p a h d", a=2, h=2)
        qnb, knb, vnb, zgb, qeb, nwTb, EBb, ktkb, vtkb, Gtb, PTb, TBb, TWb = [self.buf("g") for _ in range(13)]
        vnwb = [self.buf("vnw") for _ in range(2)]
        g3 = lambda a: a[:, 0:nch * 16].rearrange("p (n h) -> p n h", n=nch)
        def mk(which):
            def handler(ci, mw, bank, bb):
                hd = h0 + ci
                chunk = {"q": 0, "k": 16, "v": 32}[which] + hd
                cy, cyb = self.conv_chunk(tile, bank, bb, 4, self.cwg[:, li, :], self.cwg_b, chunk * 4,
                                          self.gh[:, li, :], self.ghb[li], self.ghs[:, li, :], self.ghsb[li], chunk * 3, first_tile)
                kb.op("act", lambda e: e.activation(out=cy[:, 0:T], in_=cy[:, 0:T], func=AF.Silu), reads=(cyb,), writes=(cyb,))
                if which == "v":
                    kb.op("dve", lambda e: e.tensor_copy(out=vn[:, ci, :], in_=cy[:, 0:T]), reads=(cyb,), writes=(vnb,))
                    return
                sq, sqb = self.sq[ci % 2], self.sqb[ci % 2]
                kb.op("act", lambda e: e.activation(out=sq[:], in_=cy[:, 0:T], func=AF.Square), reads=(cyb,), writes=(sqb,))
                ssb, ssbb = self.big_bank()
                self.mm(ssb[:, 0:T], self.ones_bf[:], sq[:], True, True, reads=(sqb, self.cbf_b), writes=(ssbb,), inc=True)
                rs, rsb = self.scratch()
                kb.op("act", lambda e: e.activation(out=rs[:, 0:T], in_=ssb[:, 0:T], func=AF.Sqrt, bias=self.eps_col),
                      reads=(ssbb, cb), writes=(rsb,))
                kb.op("dve", lambda e: e.reciprocal(out=rs[:, 0:T], in_=rs[:, 0:T]), reads=(rsb,), writes=(rsb,))
                if which == "q":
                    kb.op("dve", lambda e: e.scalar_tensor_tensor(out=qn[:, ci, :], in0=cy[:, 0:T], scalar=128.0 ** -0.5, in1=rs[:, 0:T],
                                                                  op0=ALU.mult, op1=ALU.mult), reads=(cyb, rsb), writes=(qnb,))
                else:
                    kb.op("dve", lambda e: e.tensor_tensor(out=kn[:, ci, :], in0=cy[:, 0:T], in1=rs[:, 0:T], op=ALU.mult),
                          reads=(cyb, rsb), writes=(knb,))
            return handler
        self.proj_fm(W, O_DQ + h0 * 128, 256, xn_list, self.xnb, mk("q"))
        self.proj_fm(W, O_DK + h0 * 128, 256, xn_list, self.xnb, mk("k"))
        self.proj_fm(W, O_DV + h0 * 128, 256, xn_list, self.xnb, mk("v"))

        def z_h(ci, mw, bank, bb):
            kb.op("act", lambda e: e.activation(out=zg[:, ci, :], in_=bank[:, 0:T], func=AF.Silu), reads=(bb,), writes=(zgb,))
        self.proj_fm(W, O_DZ + h0 * 128, 256, xn_list, self.xnb, z_h)
        for (src, srcb, dst, dstb) in ((kn, knb, k_tok, ktkb), (vn, vnb, v_tok, vtkb)):
            for n0 in range(0, nch, 2):
                nn = min(2, nch - n0)
                bank, bb = self.misc_bank()
                for j in range(nn):
                    for hh in range(2):
                        c0 = (j * 2 + hh) * 128
                        self.mm(bank[0:32, c0:c0 + 128], src[:, hh, (n0 + j) * 32:(n0 + j + 1) * 32], self.ident_bf[:], True, True,
                                reads=(srcb, self.cbf_b), writes=(bb,), inc=(j == nn - 1 and hh == 1))
                kb.op("act", lambda e, bank=bank, n0=n0, nn=nn, dst=dst: e.activation(
                    out=dst[:, n0:n0 + nn, :, :], in_=bank[0:32, 0:nn * 256].rearrange("p (n h d) -> p n h d", n=nn, h=2), func=AF.Copy),
                    reads=(bb,), writes=(dstb,))
        for n in range(nch):
            kb.op("dve", lambda e, n=n: e.tensor_tensor(
                out=Gtri[:, n, :, :], in0=Tri2.rearrange("p (h j) -> p h j", h=2),
                in1=g3(g_all)[:, n, h0:h0 + 2].unsqueeze(2).broadcast_to([32, 2, 32]), op=ALU.mult),
                reads=(gb, cb), writes=(Gtb,))
        for hh in range(2):
            bank, bb = self.misc_bank()
            for n in range(nch):
                self.mm(bank[:, n * 32:(n + 1) * 32], onesw, Gtri[:, n, hh, :], True, True,
                        reads=(cb, Gtb), writes=(bb,), inc=(n == nch - 1))
            kb.op("act", lambda e, bank=bank, hh=hh: e.activation(out=EB[:, hh, 0:nch * 32], in_=bank[:, 0:nch * 32], func=AF.Exp),
                  reads=(bb,), writes=(EBb,))
            kb.op("dve", lambda e, hh=hh: e.tensor_tensor(out=qe[:, hh, 0:nch * 32], in0=qn[:, hh, 0:nch * 32], in1=EB[:, hh, 0:nch * 32], op=ALU.mult),
                  reads=(qnb, EBb), writes=(qeb,))
        NSL = 3
        for n in range(nch):
            sl = n % NSL
            base = 10944 + sl * 448
            Ebuf = self.tf32(base, 64)[0:32, :]
            Rt = self.tf32(base + 64, 64)[0:32, :]
            PQ = [self.tf32(base + 128, 128)[0:32, :], self.tf32(base + 256, 128)[0:32, :]]
            Rm = self.tf32(base + 384, 64)[0:32, :]
            slb = self.slotb[sl]
            cols = slice(n * 32, (n + 1) * 32)
            Gn = Gtri[:, n, :, :].rearrange("p h j -> p (h j)")
            kb.op("dve", lambda e: e.tensor_tensor(
                out=Rt.rearrange("p (h j) -> p h j", h=2), in0=I2.rearrange("p (h j) -> p h j", h=2),
                in1=g3(lb_all)[:, n, h0:h0 + 2].unsqueeze(2).broadcast_to([32, 2, 32]), op=ALU.mult), reads=(gb, cb), writes=(slb,))
            kb.op("dve", lambda e: e.tensor_tensor(out=Rt, in0=Rt, in1=Gn, op=ALU.add), reads=(slb, Gtb), writes=(slb,))
            bkk, bkkb = self.misc_bank()
            for hh in range(2):
                self.mm(bkk[0:32, hh * 32:(hh + 1) * 32], kn[:, hh, cols], kn[:, hh, cols], True, True,
                        reads=(knb,), writes=(bkkb,), inc=False)
                self.mm(bkk[0:32, 64 + hh * 32:64 + (hh + 1) * 32], kn[:, hh, cols], qn[:, hh, cols], True, True,
                        reads=(knb, qnb), writes=(bkkb,), inc=(hh == 1))
            def expo(first_all, per_head, mask, dst_fn):
                bE, bEb = self.misc_bank()
                lhsT_a, rhs_a = first_all
                self.mm(bE[0:32, 0:64], lhsT_a, rhs_a, True, False, reads=(slb, Gtb, cb), writes=(bEb,), inc=False)
                for hh in range(2):
                    lh, rh = per_head(hh)
                    self.mm(bE[0:32, hh * 32:(hh + 1) * 32], lh, rh, False, False, reads=(slb, Gtb, cb), writes=(bEb,), inc=False)
                self.mm(bE[0:32, 0:64], I32, mask, False, True, reads=(cb,), writes=(bEb,), inc=True)
                kb.op("act", lambda e: e.activation(out=Ebuf, in_=bE[0:32, 0:64], func=AF.Exp), reads=(bEb,), writes=(slb,))
                dst_fn()
            expo((ones32, Rt), lambda hh: (Gtri[:, n, hh, :], neg32), M1,
                 lambda: kb.op("dve", lambda e: e.scalar_tensor_tensor(out=PQ[0][:, 0:64], in0=bkk[0:32, 0:64], scalar=-1.0, in1=Ebuf,
                                                                       op0=ALU.mult, op1=ALU.mult), reads=(bkkb, slb), writes=(slb,)))
            expo((neg32, Gn), lambda hh: (Rt[:, hh * 32:(hh + 1) * 32], ones32), M2,
                 lambda: kb.op("dve", lambda e: e.scalar_tensor_tensor(out=PQ[0][:, 64:128], in0=bkk[0:32, 0:64], scalar=-1.0, in1=Ebuf,
                                                                       op0=ALU.mult, op1=ALU.mult), reads=(bkkb, slb), writes=(slb,)))
            expo((ones32, Gn), lambda hh: (Gtri[:, n, hh, :], neg32), M3,
                 lambda: kb.op("dve", lambda e: e.tensor_tensor(out=PT[:, n, :, :].rearrange("p h j -> p (h j)"), in0=bkk[0:32, 64:128], in1=Ebuf,
                                                                op=ALU.mult), reads=(bkkb, slb), writes=(PTb,)))
            kb.op("dve", lambda e: e.tensor_tensor(out=Rm, in0=PQ[0][:, 0:64], in1=I2, op=ALU.add), reads=(slb, cb), writes=(slb,))
            cur = 0
            for lev in range(1, 5):
                nxt = 1 - cur
                bp, bpb = self.misc_bank()
                for hh in range(2):
                    Pp = PQ[cur][:, hh * 32:(hh + 1) * 32]
                    Qp = PQ[cur][:, 64 + hh * 32:64 + (hh + 1) * 32]
                    if lev < 4:
                        self.mm(bp[0:32, hh * 32:(hh + 1) * 32], Qp, Pp, True, True, reads=(slb,), writes=(bpb,), inc=False)
                    self.mm(bp[0:32, 64 + hh * 32:64 + (hh + 1) * 32], Pp, Qp, True, True, reads=(slb,), writes=(bpb,), inc=(hh == 1))
                lo = 0 if lev < 4 else 64
                kb.op("act", lambda e, nxt=nxt, lo=lo, bp=bp: e.activation(out=PQ[nxt][:, lo:128], in_=bp[0:32, lo:128], func=AF.Copy),
                      reads=(bpb,), writes=(slb,))
                br, brb = self.misc_bank()
                for hh in range(2):
                    Qn_ = PQ[nxt][:, 64 + hh * 32:64 + (hh + 1) * 32]
                    self.mm(br[0:32, hh * 32:(hh + 1) * 32], Qn_, Rm[:, hh * 32:(hh + 1) * 32], True, True, reads=(slb,), writes=(brb,), inc=(hh == 1))
                kb.op("dve", lambda e, br=br: e.tensor_tensor(out=Rm, in0=br[0:32, 0:64], in1=Rm, op=ALU.add), reads=(brb, slb), writes=(slb,))
                cur = nxt
            for hh in range(2):
                col = n * 16 + h0 + hh
                kb.op("dve", lambda e, hh=hh, col=col: e.tensor_scalar(out=TB[:, n, hh, :], in0=Rm[:, hh * 32:(hh + 1) * 32],
                                                                       scalar1=beta[:, col:col + 1], scalar2=None, op0=ALU.mult),
                      reads=(slb, gb), writes=(TBb,))
                kb.op("dve", lambda e, hh=hh, col=col: e.tensor_scalar(out=TW[:, n, hh, :], in0=Rm[:, hh * 32:(hh + 1) * 32],
                                                                       scalar1=beb[:, col:col + 1], scalar2=None, op0=ALU.mult),
                      reads=(slb, gb), writes=(TWb,))
        for hh in range(2):
            bank, bb = self.misc_bank()
            for n in range(nch):
                self.mm(bank[:, n * 32:(n + 1) * 32], k_tok[:, n, hh, :], TW[:, n, hh, :], True, True,
                        reads=(ktkb, TWb), writes=(bb,), inc=(n == nch - 1))
            kb.op("act", lambda e, bank=bank, hh=hh: e.activation(out=nwT[:, hh, 0:nch * 32], in_=bank[:, 0:nch * 32], func=AF.Copy, scale=-1.0),
                  reads=(bb,), writes=(nwTb,))
        ob = [(self.pbank[4], self.pbb[4]), (self.pbank[5], self.pbb[5])]
        b6, b6b = self.pbank[6], self.pbb[6]
        b7, b7b = self.pbank[7], self.pbb[7]
        Sv = [self.S_all[:, 1024 + (h0 + hh) * 128:1024 + (h0 + hh + 1) * 128] for hh in range(2)]
        for hh in range(2):
            kb.op("act", lambda e, hh=hh: e.activation(out=S_bf[:, hh * 128:(hh + 1) * 128], in_=Sv[hh], func=AF.Copy),
                  reads=(Sb,), writes=(S_bfb[hh],))
        for n in range(nch):
            cols = slice(n * 32, (n + 1) * 32)
            for hh in range(2):
                hd = h0 + hh
                if hs and n == npc:
                    kb.dma(self.o_st_gdn[0, li, hd], Sv[hh], reads=(Sb,), is_output=True)
                    kb.dma(Sv[hh], self.st_gdn[li, hd], writes=(Sb,))
                    kb.op("act", lambda e, hh=hh: e.activation(out=S_bf[:, hh * 128:(hh + 1) * 128], in_=Sv[hh], func=AF.Copy),
                          reads=(Sb,), writes=(S_bfb[hh],))
                Sbf = S_bf[:, hh * 128:(hh + 1) * 128]
                bv, bvb = (b6, b6b) if hh == 0 else (b7, b7b)
                self.mm(bv[0:32, 0:128], TB[:, n, hh, :], v_tok[:, n, hh, :], True, False, reads=(TBb, vtkb), writes=(bvb,), inc=False)
                self.mm(bv[0:32, 0:128], nwT[:, hh, cols], Sbf, False, True, reads=(nwTb, S_bfb[hh]), writes=(bvb,), inc=True)
                kb.op("act", lambda e, hh=hh, bv=bv: e.activation(out=vnw[:, 0, hh, :], in_=bv[0:32, 0:128], func=AF.Copy),
                      reads=(bvb,), writes=(vnwb[hh],))
                col = n * 16 + hd
                kb.op("dve", lambda e, hh=hh, bv=bv, col=col: e.tensor_scalar(out=vnw[:, 1, hh, :], in0=vnw[:, 0, hh, :],
                                                                              scalar1=dd[:, col:col + 1], scalar2=None, op0=ALU.mult),
                      reads=(vnwb[hh], gb), writes=(vnwb[hh],))
                bo, bob = ob[hh]
                self.mm(bo[:, cols], vnw[:, 0, hh, :], PT[:, n, hh, :], True, False, reads=(vnwb[hh], PTb), writes=(bob,), inc=False)
                self.mm(bo[:, cols], Sbf, qe[:, hh, cols], False, True, reads=(S_bfb[hh], qeb), writes=(bob,), inc=True)
                self.mm(bv[:, 128:256], k_tok[:, n, hh, :], vnw[:, 1, hh, :], True, True, reads=(ktkb, vnwb[hh]), writes=(bvb,), inc=True)
                kb.op("dve", lambda e, hh=hh, bv=bv, n=n: e.scalar_tensor_tensor(
                    out=Sv[hh], in0=Sv[hh], scalar=EB[:, hh, n * 32 + 31:n * 32 + 32], in1=bv[:, 128:256], op0=ALU.mult, op1=ALU.add),
                    reads=(Sb, EBb, bvb), writes=(Sb,))
                kb.op("act", lambda e, hh=hh: e.activation(out=S_bf[:, hh * 128:(hh + 1) * 128], in_=Sv[hh], func=AF.Copy),
                      reads=(Sb,), writes=(S_bfb[hh],))
        for hh in range(2):
            hd = h0 + hh
            if hs:
                kb.dma(self.o_st_gdn[1, li, hd], Sv[hh], reads=(Sb,), is_output=True)
            elif last_tile:
                kb.dma(self.o_st_gdn[0, li, hd], Sv[hh], reads=(Sb,), is_output=True)
        for hh in range(2):
            bo, bob = ob[hh]
            sq, sqb = self.sq[hh], self.sqb[hh]
            kb.op("act", lambda e, bo=bo, sq=sq: e.activation(out=sq[:], in_=bo[:, 0:T], func=AF.Square), reads=(bob,), writes=(sqb,))
            ssb, ssbb = self.big_bank()
            self.mm(ssb[:, 0:T], self.ones_bf[:], sq[:], True, True, reads=(sqb, self.cbf_b), writes=(ssbb,), inc=True)
            rs, rsb = self.scratch()
            kb.op("act", lambda e, rs=rs, ssb=ssb: e.activation(out=rs[:, 0:T], in_=ssb[:, 0:T], func=AF.Sqrt, scale=1.0 / 128, bias=self.eps_col),
                  reads=(ssbb, cb), writes=(rsb,))
            kb.op("dve", lambda e, rs=rs: e.reciprocal(out=rs[:, 0:T], in_=rs[:, 0:T]), reads=(rsb,), writes=(rsb,))
            tmp, tmpb = self.scratch()
            kb.op("dve", lambda e, bo=bo, tmp=tmp, rs=rs: e.scalar_tensor_tensor(
                out=tmp[:, 0:T], in0=bo[:, 0:T], scalar=self.sp[:, 12 + li:13 + li], in1=rs[:, 0:T], op0=ALU.mult, op1=ALU.mult),
                reads=(bob, rsb, self.spb), writes=(tmpb,))
            kb.op("dve", lambda e, tmp=tmp, hh=hh: e.tensor_tensor(out=self.mix[:, 8 + h0 + hh, :], in0=tmp[:, 0:T], in1=zg[:, hh, :], op=ALU.mult),
                  reads=(tmpb, zgb), writes=(self.mixb[8 + h0 + hh],))

    def gla_like(self, kind, li, hd, tile, first_tile, last_tile):
        kb = self.kb
        p0, npc, hs = tile
        nch = npc + (1 if hs else 0)
        W = self.w_in[li]
        xn_list = [self.xn[:, c, :] for c in range(KC)]
        if kind == "gla":
            ndv, oq, ok, ov, og = 2, O_GQ + hd * 128, O_GK + hd * 128, O_GV + hd * 256, O_GG + hd * 256
            soff, mix0, ncol = hd * 256, hd * 2, 8 + li * 2
            st_in, st_out = self.st_gla, self.o_st_gla
        else:
            ndv, oq, of_, ov, og = 1, O_HQ + hd * 128, O_HF + hd * 128, O_HI + hd * 128, O_HG + hd * 128
            soff, mix0, ncol = 3072 + hd * 128, 24 + hd, 14 + li
            st_in, st_out = self.st_hg, self.o_st_hg
        dv = 128 * ndv
        S = self.S_all[:, soff:soff + dv]
        Sb = self.Sb
        S_bf = self.tbf(0, 256)
        S_bfb = self.buf("Sbf")
        eb = self.tf32(592, T)
        enb = self.tf32(1008, T)
        cs = self.tf32(1424, T)
        fz = self.tf32(6208, T)
        qt = self.tbf(1840, T)
        kt = self.tbf(2048, T)
        sg = self.tbf(2256, 2 * T).rearrange("p (e t) -> p e t", e=2)
        v_tok = self.tbf(2672, 13 * 256).rearrange("p (n d) -> p n d", n=13)
        k_tok = self.tbf(4336, 13 * 128).rearrange("p (n d) -> p n d", n=13)
        AT = self.tbf(5168, T)
        ebb, csb, fzb, qtb, ktb, sgb, vtb, ktkb, ATb = [self.buf("t") for _ in range(9)]
        maskT = self.consts_sb[0:32, 512:512 + T]
        rmask = self.consts_sb[:, 928:928 + T]
        b4, b4b = self.pbank[4], self.pbb[4]
        b5, b5b = self.pbank[5], self.pbb[5]
        b6, b6b = self.pbank[6], self.pbb[6]
        b7, b7b = self.pbank[7], self.pbb[7]
        ob = [(b4, b4b), (b5, b5b)]
        if kind == "gla":
            self.mm(b6[:, 0:T], self.wgg_bf[0:16, hd * 128:(hd + 1) * 128], self.lr_bf[0:16, :], True, True,
                    reads=(self.lrb,), writes=(b6b,), inc=True)
            kb.op("act", lambda e: e.activation(out=fz, in_=b6[:, 0:T], func=AF.Exp, scale=-1.0,
                                                bias=self.negb[:, li * 4 + hd:li * 4 + hd + 1]),
                  reads=(b6b, self.spb), writes=(fzb,))
            kb.op("act", lambda e: e.activation(out=fz, in_=fz, func=AF.Ln, bias=self.one_col), reads=(fzb, self.consts_b), writes=(fzb,))
            kb.op("dve", lambda e: e.tensor_tensor_scan(out=cs, data0=rmask, data1=fz, initial=0.0, op0=ALU.mult, op1=ALU.add),
                  reads=(fzb, self.consts_b), writes=(csb,))
            kb.op("act", lambda e: e.activation(out=eb, in_=cs, func=AF.Exp, scale=-1.0 / 16.0), reads=(csb,), writes=(ebb,))
            kb.op("act", lambda e: e.activation(out=enb, in_=cs, func=AF.Exp, scale=1.0 / 16.0), reads=(csb,), writes=(ebb,))
        else:
            def f_h(ci, mw, bank, bb):
                kb.op("act", lambda e: e.activation(out=fz, in_=bank[:, 0:T], func=AF.Sigmoid), reads=(bb,), writes=(fzb,))
            self.proj_fm(W, of_, 128, xn_list, self.xnb, f_h)
            kb.op("dve", lambda e: e.tensor_scalar(out=eb, in0=fz, scalar1=self.oml[:, hd:hd + 1], scalar2=self.lbv[:, hd:hd + 1],
                                                   op0=ALU.mult, op1=ALU.add), reads=(fzb, self.spb), writes=(ebb,))
            kb.op("act", lambda e: e.activation(out=eb, in_=eb, func=AF.Ln), reads=(ebb,), writes=(ebb,))
            kb.op("dve", lambda e: e.tensor_tensor_scan(out=cs, data0=rmask, data1=eb, initial=0.0, op0=ALU.mult, op1=ALU.add),
                  reads=(ebb, self.consts_b), writes=(csb,))
            kb.op("act", lambda e: e.activation(out=eb, in_=cs, func=AF.Exp), reads=(csb,), writes=(ebb,))
            kb.op("act", lambda e: e.activation(out=enb, in_=cs, func=AF.Exp, scale=-1.0), reads=(csb,), writes=(ebb,))
            kb.op("dve", lambda e: e.tensor_scalar(out=fz, in0=fz, scalar1=self.noml[:, hd:hd + 1], scalar2=self.oml[:, hd:hd + 1],
                                                   op0=ALU.mult, op1=ALU.add), reads=(fzb, self.spb), writes=(fzb,))
        if kind == "gla":
            def q_h(ci, mw, bank, bb):
                kb.op("dve", lambda e: e.scalar_tensor_tensor(out=qt, in0=bank[:, 0:T], scalar=128.0 ** -0.5, in1=eb,
                                                              op0=ALU.mult, op1=ALU.mult), reads=(bb, ebb), writes=(qtb,))
            self.proj_fm(W, oq, 128, xn_list, self.xnb, q_h)

            def k_h(ci, mw, bank, bb):
                kb.op("dve", lambda e: e.tensor_tensor(out=kt, in0=bank[:, 0:T], in1=enb, op=ALU.mult), reads=(bb, ebb), writes=(ktb,))
            self.proj_fm(W, ok, 128, xn_list, self.xnb, k_h)
        else:
            def q_h(ci, mw, bank, bb):
                tmp, tmpb = self.scratch()
                kb.op("act", lambda e: e.activation(out=tmp[:, 0:T], in_=bank[:, 0:T], func=AF.Silu), reads=(bb,), writes=(tmpb,))
                kb.op("dve", lambda e: e.tensor_tensor(out=qt, in0=tmp[:, 0:T], in1=eb, op=ALU.mult), reads=(tmpb, ebb), writes=(qtb,))
            self.proj_fm(W, oq, 128, xn_list, self.xnb, q_h)
            kb.op("dve", lambda e: e.tensor_tensor(out=kt, in0=fz, in1=enb, op=ALU.mult), reads=(fzb, ebb), writes=(ktb,))
        wv, wb = self.wload(W[:, ov:ov + dv], KC, dv)
        for n in range(nch):
            bank, bb = self.big_bank()
            for k in range(KC):
                self.mm(bank[0:32, 0:dv], self.xn[:, k, n * 32:(n + 1) * 32], wv[:, k, :], k == 0, k == KC - 1,
                        reads=(wb, self.xnb[k]), writes=(bb,), inc=(k == KC - 1))
            kb.op("act", lambda e, n=n, bank=bank: e.activation(out=v_tok[0:32, n, 0:dv], in_=bank[0:32, 0:dv], func=AF.Copy),
                  reads=(bb,), writes=(vtb,))
        for n0 in range(0, nch, 4):
            nn = min(4, nch - n0)
            for j in range(nn):
                n = n0 + j
                self.mm(b6[0:32, j * 128:(j + 1) * 128], kt[:, n * 32:(n + 1) * 32], self.ident_bf[:], True, True,
                        reads=(ktb, self.cbf_b), writes=(b6b,), inc=(j == nn - 1))
            kb.op("act", lambda e, n0=n0, nn=nn: e.activation(
                out=k_tok[0:32, n0:n0 + nn, :], in_=b6[0:32, 0:nn * 128].rearrange("p (n d) -> p n d", n=nn), func=AF.Copy),
                reads=(b6b,), writes=(ktkb,))
        for n in range(nch):
            self.mm(b6[0:32, n * 32:(n + 1) * 32], kt[:, n * 32:(n + 1) * 32], qt[:, n * 32:(n + 1) * 32], True, True,
                    reads=(ktb, qtb), writes=(b6b,), inc=(n == nch - 1))
        kb.op("dve", lambda e: e.tensor_tensor(out=AT[0:32, 0:nch * 32], in0=b6[0:32, 0:nch * 32], in1=maskT[:, 0:nch * 32], op=ALU.mult),
              reads=(b6b, self.consts_b), writes=(ATb,))
        def g_h(ci, mw, bank, bb):
            kb.op("act", lambda e: e.activation(out=sg[:, ci, :], in_=bank[:, 0:T], func=AF.Silu), reads=(bb,), writes=(sgb,))
        self.proj_fm(W, og, dv, xn_list, self.xnb, g_h)
        if kind == "gla" and li == 0 and hd == 0:
            self._dump_reads = [fzb, csb, ebb, qtb, ktb, ATb, vtb, ktkb]
            self.dump(0, fz); self.dump(1, cs); self.dump(2, eb); self.dump(3, enb); self.dump(4, qt); self.dump(5, kt)
            self.dump(6, AT[0:32, :], rows=32); self.dump(7, v_tok[0:32, 12, :], rows=32, cols=256); self.dump(8, k_tok[0:32, 12, :], rows=32, cols=128)
        kb.op("act", lambda e: e.activation(out=S_bf[:, 0:dv], in_=S, func=AF.Copy), reads=(Sb,), writes=(S_bfb,))
        for n in range(nch):
            if hs and n == npc:
                kb.dma(st_out[0, li, hd], S, reads=(Sb,), is_output=True)
                kb.dma(S, st_in[li, hd], writes=(Sb,))
                kb.op("act", lambda e: e.activation(out=S_bf[:, 0:dv], in_=S, func=AF.Copy), reads=(Sb,), writes=(S_bfb,))
            cols = slice(n * 32, (n + 1) * 32)
            for e_ in range(ndv):
                bank, bb = ob[e_]
                self.mm(bank[:, cols], v_tok[0:32, n, e_ * 128:(e_ + 1) * 128], AT[0:32, cols], True, False,
                        reads=(vtb, ATb), writes=(bb,), inc=False)
                self.mm(bank[:, cols], S_bf[:, e_ * 128:(e_ + 1) * 128], qt[:, cols], False, True,
                        reads=(S_bfb, qtb), writes=(bb,), inc=True)
            self.mm(b7[:, 0:dv], k_tok[0:32, n, :], v_tok[0:32, n, 0:dv], True, True,
                    reads=(ktkb, vtb), writes=(b7b,), inc=True)
            kb.op("dve", lambda e: e.tensor_tensor(out=S, in0=b7[:, 0:dv], in1=S, op=ALU.add), reads=(b7b, Sb), writes=(Sb,))
            kb.op("dve", lambda e, n=n: e.tensor_scalar(out=S, in0=S, scalar1=eb[:, n * 32 + 31:n * 32 + 32], scalar2=None, op0=ALU.mult),
                  reads=(Sb, ebb), writes=(Sb,))
            kb.op("act", lambda e: e.activation(out=S_bf[:, 0:dv], in_=S, func=AF.Copy), reads=(Sb,), writes=(S_bfb,))
        if hs:
            kb.dma(st_out[1, li, hd], S, reads=(Sb,), is_output=True)
        elif last_tile:
            kb.dma(st_out[0, li, hd], S, reads=(Sb,), is_output=True)
        ssb, ssbb = self.big_bank()
        for e_ in range(ndv):
            bank, bb = ob[e_]
            sq, sqb = self.sq[e_ % 2], self.sqb[e_ % 2]
            kb.op("act", lambda e, bank=bank, sq=sq: e.activation(out=sq[:], in_=bank[:, 0:T], func=AF.Square), reads=(bb,), writes=(sqb,))
            self.mm(ssb[:, 0:T], self.ones_bf[:], sq[:], e_ == 0, e_ == ndv - 1, reads=(sqb, self.cbf_b), writes=(ssbb,), inc=True)
        rs, rsb = self.scratch()
        kb.op("act", lambda e: e.activation(out=rs[:, 0:T], in_=ssb[:, 0:T], func=AF.Sqrt, scale=1.0 / dv, bias=self.eps_col),
              reads=(ssbb, self.consts_b), writes=(rsb,))
        kb.op("dve", lambda e: e.reciprocal(out=rs[:, 0:T], in_=rs[:, 0:T]), reads=(rsb,), writes=(rsb,))
        for e_ in range(ndv):
            bank, bb = ob[e_]
            tmp, tmpb = self.scratch()
            kb.op("dve", lambda e, bank=bank, tmp=tmp, e_=e_: e.scalar_tensor_tensor(
                out=tmp[:, 0:T], in0=bank[:, 0:T], scalar=self.sp[:, ncol + e_:ncol + e_ + 1], in1=rs[:, 0:T], op0=ALU.mult, op1=ALU.mult),
                reads=(bb, rsb, self.spb), writes=(tmpb,))
            kb.op("dve", lambda e, tmp=tmp, e_=e_: e.tensor_tensor(out=self.mix[:, mix0 + e_, :], in0=tmp[:, 0:T], in1=sg[:, e_, :], op=ALU.mult),
                  reads=(tmpb, sgb), writes=(self.mixb[mix0 + e_],))


def fm_vec(v):
    v = np.asarray(v, dtype=np.float32)
    return np.ascontiguousarray(v.reshape(-1, 128).T)


def make_consts():
    c = np.zeros((128, NCONST), np.float32)
    c[:, 0:128] = np.eye(128, dtype=np.float32)
    c[:, 128:256] = 1.0
    c[:, 256] = EPS
    c[:, 257] = 1.0
    for n in range(NCH):
        for s_ in range(32):
            c[s_, 512 + n * 32 + s_:512 + (n + 1) * 32] = 1.0
    c[:, 928:928 + T] = 1.0
    c[:, 928:928 + T:32] = 0.0
    ii = np.arange(32)[:, None]
    jj = np.arange(32)[None, :]
    tri = (ii <= jj).astype(np.float32)
    c[0:32, 1344:1376] = tri
    c[0:32, 1376:1408] = (ii > jj).astype(np.float32)
    c[0:32, 1408:1440] = -1.0
    c[0:32, 1440:1504] = np.tile(np.where(ii >= jj, -30000.0, 0.0), (1, 2))
    c[0:32, 1504:1568] = np.tile(np.where(jj >= ii, -30000.0, 0.0), (1, 2))
    c[0:32, 1568:1632] = np.tile(np.where(ii > jj, -30000.0, 0.0), (1, 2))
    c[0:32, 1632:1696] = np.tile(np.eye(32, dtype=np.float32), (1, 2))
    c[0:32, 1696:1760] = np.tile(tri, (1, 2))
    return c


def build_program(cfg):
    nc = bass.Bass("TRN2", target_bir_lowering=False)
    b = Builder(nc, cfg)
    b.build()
    return nc


def prepare_inputs(inp, cfg):
    f = lambda a: np.ascontiguousarray(np.asarray(a, dtype=np.float32))
    xp = f(inp["x_prompt"])
    xs = f(inp["x_sample"])
    pp = f(inp["p_prompt"])
    ps = f(inp["p_sample"])
    vecs = np.concatenate([fm_vec(inp["norm_mix"][0]), fm_vec(inp["norm_mix"][1]),
                           fm_vec(inp["norm_ffn"][0]), fm_vec(inp["norm_ffn"][1]),
                           fm_vec(inp["norm_ple"][0]), fm_vec(inp["norm_ple"][1]),
                           fm_vec(inp["norm_final"])], axis=1)
    cw_ffn = np.stack([np.ascontiguousarray(f(inp["w_ffn_conv"][l]).T.reshape(172, 128, 3).transpose(1, 0, 2)).reshape(128, 172 * 3)
                       for l in range(DEPTH)])
    cw_gdn = np.stack([np.ascontiguousarray(f(inp["w_gdn_conv"][l]).T.reshape(48, 128, 4).transpose(1, 0, 2)).reshape(128, 48 * 4)
                       for l in range(DEPTH)])
    smallp = np.zeros((128, 64), np.float32)
    smallp[:, 0:4] = fm_vec(inp["b_gla_gate"][0]); smallp[:, 4:8] = fm_vec(inp["b_gla_gate"][1])
    smallp[:, 8:10] = fm_vec(inp["gla_norm"][0]); smallp[:, 10:12] = fm_vec(inp["gla_norm"][1])
    smallp[:, 12:13] = fm_vec(inp["gdn_norm"][0]); smallp[:, 13:14] = fm_vec(inp["gdn_norm"][1])
    smallp[:, 14:15] = fm_vec(inp["hgrn_norm"][0]); smallp[:, 15:16] = fm_vec(inp["hgrn_norm"][1])
    hlb = np.concatenate([fm_vec(inp["hgrn_lb"][0]), fm_vec(inp["hgrn_lb"][1])], axis=1)
    gdnrow = np.zeros((128, 832), np.float32)
    for l in range(DEPTH):
        gdnrow[:, l * 416:l * 416 + 208] = np.tile(f(inp["gdn_a_log"][l]), 13)[None, :]
        gdnrow[:, l * 416 + 208:l * 416 + 416] = np.tile(f(inp["gdn_dt_bias"][l]), 13)[None, :]
    consts = make_consts()
    shared = {
        "w_in": f(inp["w_in"]), "w_out": f(inp["w_out"]), "w_up": f(inp["w_up"]), "w_down": f(inp["w_down"]),
        "w_pg": f(inp["w_ple_gate"]), "w_pp": f(inp["w_ple_proj"]), "w_gg": f(inp["w_gla_gate"]),
        "vecs": vecs, "cw_ffn": cw_ffn, "cw_gdn": cw_gdn, "smallp": smallp, "hlb": hlb, "gdnrow": gdnrow,
        "consts": consts,
    }
    maps = []
    for c in range(8):
        b = c % 4
        xin = np.concatenate([xp[b].T, xs[c].T], axis=1)
        pin = np.concatenate([pp[:, b].transpose(0, 2, 1), ps[:, c].transpose(0, 2, 1)], axis=2)
        cg = f(inp["cache_gdn_conv"][:, c])
        cg = cg.transpose(0, 2, 1).reshape(DEPTH, 48, 128, 3).transpose(0, 2, 1, 3).reshape(DEPTH, 128, 48 * 3)
        cf = f(inp["cache_ffn_conv"][:, c])
        cf = cf.transpose(0, 2, 1).reshape(DEPTH, 172, 128, 2).transpose(0, 2, 1, 3).reshape(DEPTH, 128, 172 * 2)
        m = dict(shared)
        m.update({
            "xin": np.ascontiguousarray(xin), "pin": np.ascontiguousarray(pin),
            "st_gla": f(inp["state_gla"][:, c]), "st_gdn": f(inp["state_gdn"][:, c]), "st_hg": f(inp["state_hgrn"][:, c]),
            "cg_in": np.ascontiguousarray(cg), "cf_in": np.ascontiguousarray(cf),
        })
        maps.append(m)
    return maps


def assemble(results):
    y_p = np.stack([results[b]["yout"][:, 0:SEQ].T for b in range(4)])
    y_s = np.stack([results[c]["yout"][:, SEQ:NCOL].T for c in range(8)])

    def st(name, grp, n):
        return np.stack([results[c][name][grp] for c in range(n)], axis=1)

    def cache(name, grp, n, nchunk, w):
        outs = []
        for c in range(n):
            a = results[c][name][grp]
            a = a.reshape(DEPTH, 128, nchunk, w).transpose(0, 3, 2, 1).reshape(DEPTH, w, nchunk * 128)
            outs.append(a)
        return np.stack(outs, axis=1)
    out = (y_p, y_s,
           st("o_st_gla", 0, 4), st("o_st_gdn", 0, 4), cache("o_cg", 0, 4, 48, 3), st("o_st_hg", 0, 4), cache("o_cf", 0, 4, 172, 2),
           st("o_st_gla", 1, 8), st("o_st_gdn", 1, 8), cache("o_cg", 1, 8, 48, 3), st("o_st_hg", 1, 8), cache("o_cf", 1, 8, 172, 2))
    return tuple(np.ascontiguousarray(o.astype(np.float32)) for o in out)


def kernel(**inputs):
    nc = build_program(CFG)
    maps = prepare_inputs(inputs, CFG)
    res = run_bass_kernel_spmd(nc, maps, core_ids=list(range(8)))
    return assemble(res.results)
```

```python
import numpy as np
import concourse.bass as bass
import concourse.mybir as mybir
from concourse.bass_utils import run_bass_kernel_spmd

F32 = mybir.dt.float32
BF16 = mybir.dt.bfloat16
AF = mybir.ActivationFunctionType
ALU = mybir.AluOpType

D = 4096
KC = 32
T = 416
C = 32
NCH = 13
DEPTH = 2
D_FF = 11008
NJ = 86
N_IN = 15408
EPS = 1e-6
SEQ = 2048
NCOL = SEQ + 32
SLOT = 8192
NSLOT = 3
NCONST = 1792

TILES_FULL = [(0, 13, False), (416, 13, False), (832, 13, False), (1248, 13, False), (1664, 12, True)]
CFG = {"tiles": TILES_FULL, "mixers": True, "ffn": True, "ple": True}

O_GQ, O_GK, O_GV, O_GG, O_GLR = 0, 512, 1024, 2048, 3072
O_DQ, O_DK, O_DV, O_DZ, O_DB, O_DA = 3088, 5136, 7184, 9232, 11280, 11296
O_HQ, O_HF, O_HI, O_HG = 11312, 12336, 13360, 14384


class Tok:
    __slots__ = ("sem", "key", "val")

    def __init__(self, sem, key, val):
        self.sem = sem
        self.key = key
        self.val = val


class Buf:
    __slots__ = ("name", "w", "r")

    def __init__(self, name):
        self.name = name
        self.w = None
        self.r = {}


class Eng:
    def __init__(self, nc, e, name, key):
        self.e = e
        self.name = name
        self.key = key
        self.sem = nc.alloc_semaphore("sem_" + name)
        self.count = 0
        self.seen = {}


class KB:
    def __init__(self, nc):
        self.nc = nc
        self.E = {
            "pe": Eng(nc, nc.tensor, "pe", 0),
            "act": Eng(nc, nc.scalar, "act", 1),
            "dve": Eng(nc, nc.vector, "dve", 2),
            "pool": Eng(nc, nc.gpsimd, "pool", 3),
            "sp": Eng(nc, nc.sync, "sp", 4),
        }
        self.nds = 20
        self.dsem = [nc.alloc_semaphore(f"dsem{i}") for i in range(self.nds)]
        self.dcount = 0
        self.wsem = [nc.alloc_semaphore(f"wsem{i}") for i in range(NSLOT)]
        self.wcount = [0] * NSLOT
        self.out_toks = []
        self.all_dma = []

    def _wait(self, E, toks):
        for t in toks:
            if t is None:
                continue
            if E.seen.get(t.key, 0) >= t.val:
                continue
            E.e.wait_ge(t.sem, t.val)
            E.seen[t.key] = t.val

    def _deps(self, E, reads, writes):
        deps = []
        for b in reads:
            if b.w is not None:
                deps.append(b.w)
        for b in writes:
            if b.w is not None and b.w.key != E.key:
                deps.append(b.w)
            for t in b.r.values():
                if t.key != E.key:
                    deps.append(t)
        return deps

    def _record(self, tok, reads, writes):
        for b in reads:
            o = b.r.get(tok.key)
            if o is None or o.val < tok.val:
                b.r[tok.key] = tok
        for b in writes:
            b.w = tok
            b.r = {}

    def op(self, eng, fn, reads=(), writes=(), inc=True):
        E = self.E[eng]
        self._wait(E, self._deps(E, reads, writes))
        ins = fn(E.e)
        if inc:
            E.count += 1
            ins.then_inc(E.sem, 1)
            tok = Tok(E.sem, E.key, E.count)
        else:
            tok = Tok(E.sem, E.key, E.count + 1)
        self._record(tok, reads, writes)
        return tok

    def dma(self, out, in_, reads=(), writes=(), q="sp", is_output=False):
        E = self.E[q]
        n = self.dcount
        self.dcount += 1
        j = n % self.nds
        sem = self.dsem[j]
        key = 100 + j
        prev = n // self.nds
        deps = self._deps(E, reads, writes)
        if prev > 0:
            deps.append(Tok(sem, key, 16 * prev))
        self._wait(E, deps)
        E.e.dma_start(out=out, in_=in_).then_inc(sem, 16)
        tok = Tok(sem, key, 16 * (prev + 1))
        self._record(tok, reads, writes)
        self.all_dma.append(tok)
        if is_output:
            self.out_toks.append(tok)
        return tok

    def finish(self):
        E = self.E["sp"]
        self._wait(E, self.out_toks)


class Builder:
    def __init__(self, nc, cfg):
        self.nc = nc
        self.cfg = cfg
        self.kb = KB(nc)
        self.wnext = 0
        self.nbig = 0
        self.nmisc = 0
        self._uid = 0

    def sb(self, name, shape, dt):
        t = self.nc.alloc_sbuf_tensor(name, list(shape), dt)
        return t.ap()

    def buf(self, name="b"):
        self._uid += 1
        return Buf(f"{name}{self._uid}")

    def declare_io(self):
        nc = self.nc

        def din(name, shape):
            return nc.dram_tensor(name, list(shape), F32, kind="ExternalInput").ap()

        def dout(name, shape):
            return nc.dram_tensor(name, list(shape), F32, kind="ExternalOutput").ap()

        self.xin = din("xin", [D, NCOL])
        self.pin = din("pin", [DEPTH, 256, NCOL])
        self.st_gla = din("st_gla", [DEPTH, 4, 128, 256])
        self.st_gdn = din("st_gdn", [DEPTH, 16, 128, 128])
        self.st_hg = din("st_hg", [DEPTH, 8, 128, 128])
        self.cg_in = din("cg_in", [DEPTH, 128, 48 * 3])
        self.cf_in = din("cf_in", [DEPTH, 128, 172 * 2])
        self.w_in = din("w_in", [DEPTH, D, N_IN])
        self.w_out = din("w_out", [DEPTH, D, D])
        self.w_up = din("w_up", [DEPTH, D, 2 * D_FF])
        self.w_down = din("w_down", [DEPTH, D_FF, D])
        self.w_pg = din("w_pg", [DEPTH, D, D])
        self.w_pp = din("w_pp", [DEPTH, 256, D])
        self.w_gg = din("w_gg", [DEPTH, 16, 512])
        self.vecs = din("vecs", [128, 7 * 32])
        self.cw_ffn = din("cw_ffn", [DEPTH, 128, 172 * 3])
        self.cw_gdn = din("cw_gdn", [DEPTH, 128, 48 * 4])
        self.smallp = din("smallp", [128, 64])
        self.hlb = din("hlb", [128, 16])
        self.gdnrow = din("gdnrow", [128, 832])
        self.consts = din("consts", [128, NCONST])
        self.hspill = nc.dram_tensor("hspill", [128, KC * T], F32, kind="Internal").ap()
        self.st_scr = nc.dram_tensor("st_scr", [DEPTH, 128, 4096], F32, kind="Internal").ap()
        self.dbg = dout("dbg", [16, 128, T]) if self.cfg.get("dbg") else None
        self.yout = dout("yout", [D, NCOL])
        self.o_st_gla = dout("o_st_gla", [2, DEPTH, 4, 128, 256])
        self.o_st_gdn = dout("o_st_gdn", [2, DEPTH, 16, 128, 128])
        self.o_st_hg = dout("o_st_hg", [2, DEPTH, 8, 128, 128])
        self.o_cg = dout("o_cg", [2, DEPTH, 128, 48 * 3])
        self.o_cf = dout("o_cf", [2, DEPTH, 128, 172 * 2])

    def alloc(self):
        nc = self.nc
        self.h = self.sb("h", [128, KC, T], F32)
        self.hflat = self.h.rearrange("p k t -> p (k t)")
        self.hb = [self.buf("h") for _ in range(KC)]
        self.xn = self.sb("xn", [128, KC, T], BF16)
        self.xnb = [self.buf("xn") for _ in range(KC)]
        self.ARENA = 21504
        self.arena = self.sb("arena", [128, self.ARENA], BF16)
        self.mix = self.arena[:, 0:KC * T].rearrange("p (k t) -> p k t", k=KC)
        self.mixb = [self.buf("mix") for _ in range(KC)]
        self.act = self.arena[:, 0:43 * T].rearrange("p (k t) -> p k t", k=43)
        self.actb = [self.buf("act") for _ in range(43)]
        self.ring = [self.sb(f"ring{i}", [128, SLOT], BF16) for i in range(NSLOT)]
        self.ringb = [self.buf("ring") for _ in range(NSLOT)]
        self.pbank = [nc.alloc_psum_tensor(f"pb{i}", [128, 512], F32).ap() for i in range(8)]
        self.pbb = [self.buf("pb") for _ in range(8)]
        self.vecs_sb = self.sb("vecs_sb", [128, 7 * 32], F32)
        self.vecs_b = self.buf("vecs")
        self.consts_sb = self.sb("consts_sb", [128, NCONST], F32)
        self.consts_b = self.buf("consts")
        self.ones_bf = self.sb("ones_bf", [128, 128], BF16)
        self.ident_bf = self.sb("ident_bf", [128, 128], BF16)
        self.cbf_b = self.buf("cbf")
        self.sq = [self.sb(f"sq{i}", [128, T], BF16) for i in range(2)]
        self.sqb = [self.buf("sq") for _ in range(2)]
        self.rstd = self.sb("rstd", [128, T], F32)
        self.rstd_b = self.buf("rstd")
        self.cwf = self.sb("cwf", [128, DEPTH, 172 * 3], F32)
        self.cwf_b = self.buf("cwf")
        self.fh = self.sb("fh", [128, DEPTH, 172 * 2], F32)
        self.fhb = [self.buf("fh") for _ in range(DEPTH)]
        self.fhs = self.sb("fhs", [128, DEPTH, 172 * 2], F32)
        self.fhsb = [self.buf("fhs") for _ in range(DEPTH)]
        self.NSCR = 4
        self.scr = [self.sb(f"scr{i}", [128, T + 4], F32) for i in range(self.NSCR)]
        self.scrb = [self.buf("scr") for _ in range(self.NSCR)]
        self.nscr = 0
        self.NLNG = 5
        self.lng = [self.sb(f"lng{i}", [128, T], F32) for i in range(self.NLNG)]
        self.lngb = [self.buf("lng") for _ in range(self.NLNG)]
        self.nlng = 0
        self.pe_bf = self.arena[:, 0:2 * T].rearrange("p (k t) -> p k t", k=2)
        self.pebf_b = self.buf("pe_bf")
        self.cwg = self.sb("cwg", [128, DEPTH, 48 * 4], F32)
        self.cwg_b = self.buf("cwg")
        self.gh = self.sb("gh", [128, DEPTH, 48 * 3], F32)
        self.ghb = [self.buf("gh") for _ in range(DEPTH)]
        self.ghs = self.sb("ghs", [128, DEPTH, 48 * 3], F32)
        self.ghsb = [self.buf("ghs") for _ in range(DEPTH)]
        self.grow_b = self.buf("grow")
        self.slotb = [self.buf("slot") for _ in range(4)]

    def scratch(self):
        i = self.nscr % self.NSCR
        self.nscr += 1
        return self.scr[i], self.scrb[i]

    def scratch_long(self):
        i = self.nlng % self.NLNG
        self.nlng += 1
        return self.lng[i], self.lngb[i]

    def big_bank(self):
        i = self.nbig % 4
        self.nbig += 1
        return self.pbank[i], self.pbb[i]

    def misc_bank(self):
        i = 4 + self.nmisc % 4
        self.nmisc += 1
        return self.pbank[i], self.pbb[i]

    def wload(self, src, nk, ncols):
        kb = self.kb
        i = self.wnext % NSLOT
        self.wnext += 1
        assert nk * ncols <= SLOT
        dst = self.ring[i][:, 0:nk * ncols].rearrange("p (k n) -> p k n", k=nk)
        b = self.ringb[i]
        E = kb.E["pool"]
        kb._wait(E, kb._deps(E, (), (b,)))
        sem = kb.wsem[i]
        kb.wcount[i] += 1
        if nk == 1:
            E.e.dma_start(out=self.ring[i][:, 0:ncols], in_=src).then_inc(sem, 16)
        else:
            E.e.dma_start(out=dst, in_=src.rearrange("(k p) n -> p k n", p=128)).then_inc(sem, 16)
        tok = Tok(sem, 200 + i, 16 * kb.wcount[i])
        kb._record(tok, (), (b,))
        return dst, b

    def mm(self, out, lhsT, rhs, start, stop, reads, writes, inc):
        return self.kb.op("pe", lambda e: e.matmul(out, lhsT=lhsT, rhs=rhs, start=start, stop=stop),
                          reads=reads, writes=writes, inc=inc)

    def proj_fm(self, W, c0, ncols, rhs, rhsb, handler, k0=0, nk=KC, ucols=256):
        ucols = min(ucols, SLOT // nk)
        col = 0
        ci = 0
        while col < ncols:
            uc = min(ucols, ncols - col)
            wv, wb = self.wload(W[k0 * 128:(k0 + nk) * 128, c0 + col:c0 + col + uc], nk, uc)
            m0 = 0
            while m0 < uc:
                mw = min(128, uc - m0)
                bank, bb = self.big_bank()
                for k in range(nk):
                    self.mm(bank[0:mw, 0:T], wv[:, k, m0:m0 + mw], rhs[k], k == 0, k == nk - 1,
                            reads=(wb, rhsb[k]), writes=(bb,), inc=(k == nk - 1))
                handler(ci, mw, bank, bb)
                ci += 1
                m0 += mw
            col += uc

    def norm(self, wcol0, out_fn):
        kb = self.kb
        bank, bb = self.misc_bank()
        for c in range(KC):
            s = c % 2
            kb.op("act", lambda e, c=c, s=s: e.activation(out=self.sq[s][:], in_=self.h[:, c, :], func=AF.Square),
                  reads=(self.hb[c],), writes=(self.sqb[s],))
            self.mm(bank[:, 0:T], self.ones_bf[:], self.sq[s][:], c == 0, c == KC - 1,
                    reads=(self.sqb[s], self.cbf_b), writes=(bb,), inc=True)
        kb.op("act", lambda e: e.activation(out=self.rstd[:], in_=bank[:, 0:T], func=AF.Sqrt,
                                            scale=1.0 / D, bias=self.eps_col),
              reads=(bb, self.consts_b), writes=(self.rstd_b,))
        kb.op("dve", lambda e: e.reciprocal(out=self.rstd[:], in_=self.rstd[:]),
              reads=(self.rstd_b,), writes=(self.rstd_b,))
        for c in range(KC):
            out_fn(c, self.vecs_sb[:, wcol0 + c:wcol0 + c + 1])

    def norm_to_xn(self, wcol0):
        kb = self.kb

        def f(c, wcol):
            kb.op("dve", lambda e: e.scalar_tensor_tensor(out=self.xn[:, c, :], in0=self.h[:, c, :], scalar=wcol,
                                                          in1=self.rstd[:], op0=ALU.mult, op1=ALU.mult),
                  reads=(self.hb[c], self.rstd_b, self.vecs_b), writes=(self.xnb[c],))
        self.norm(wcol0, f)

    def segs(self, tile):
        p0, npc, hs = tile
        s = [("p", 0, npc * C)]
        if hs:
            s.append(("s", npc * C, C))
        return s

    def conv_chunk(self, tile, bank, bb, W, wtile, wbuf, wbase, hist_p, hist_pb, hist_s, hist_sb, hbase, first_tile):
        kb = self.kb
        H = W - 1
        cy, cyb = self.scratch_long()
        for (kind, col0, n) in self.segs(tile):
            st, stb = self.scratch()
            hist, histb = (hist_p, hist_pb) if kind == "p" else (hist_s, hist_sb)
            hv = hist[:, hbase:hbase + H]
            if kind == "p" and first_tile:
                kb.op("dve", lambda e, st=st: e.memset(st[:, 0:H], 0.0), reads=(), writes=(stb,))
            else:
                kb.op("dve", lambda e, st=st, hv=hv: e.tensor_copy(out=st[:, 0:H], in_=hv),
                      reads=(histb,), writes=(stb,))
            kb.op("act", lambda e, st=st, col0=col0, n=n: e.activation(out=st[:, H:H + n], in_=bank[:, col0:col0 + n],
                                                                    func=AF.Copy),
                  reads=(bb, stb), writes=(stb,))
            kb.op("dve", lambda e, st=st, hv=hv, n=n: e.tensor_copy(out=hv, in_=st[:, n:n + H]),
                  reads=(stb,), writes=(histb,))
            for j in range(W - 1, -1, -1):
                wc = wtile[:, wbase + j:wbase + j + 1]
                if j == W - 1:
                    kb.op("dve", lambda e, st=st, wc=wc, j=j, n=n, col0=col0: e.tensor_scalar(
                        out=cy[:, col0:col0 + n], in0=st[:, j:j + n], scalar1=wc, scalar2=None, op0=ALU.mult),
                        reads=(stb, wbuf), writes=(cyb,))
                else:
                    kb.op("dve", lambda e, st=st, wc=wc, j=j, n=n, col0=col0: e.scalar_tensor_tensor(
                        out=cy[:, col0:col0 + n], in0=st[:, j:j + n], scalar=wc, in1=cy[:, col0:col0 + n],
                        op0=ALU.mult, op1=ALU.add),
                        reads=(stb, wbuf, cyb), writes=(cyb,))
        return cy, cyb

    def ffn(self, li, tile, first_tile):
        kb = self.kb
        Wu = self.w_up[li]
        Wd = self.w_down[li]
        xn_list = [self.xn[:, c, :] for c in range(KC)]
        for half in range(2):
            j0 = half * 43
            for jp in range(0, 43, 2):
                nj = min(2, 43 - jp)
                res = {}

                def hnd(which, jp=jp):
                    def handler(ci, mw, bank, bb):
                        j = j0 + jp + ci
                        chunk = j if which == "g" else NJ + j
                        cy, cyb = self.conv_chunk(tile, bank, bb, 3, self.cwf[:, li, :], self.cwf_b, chunk * 3,
                                                  self.fh[:, li, :], self.fhb[li], self.fhs[:, li, :], self.fhsb[li],
                                                  chunk * 2, first_tile)
                        res[(which, ci)] = (cy, cyb)
                    return handler
                self.proj_fm(Wu, (j0 + jp) * 128, nj * 128, xn_list, self.xnb, hnd("g"))
                self.proj_fm(Wu, D_FF + (j0 + jp) * 128, nj * 128, xn_list, self.xnb, hnd("v"))
                for ci in range(nj):
                    jj = jp + ci
                    g, gb = res[("g", ci)]
                    v, vb = res[("v", ci)]
                    kb.op("act", lambda e, g=g: e.activation(out=g[:, 0:T], in_=g[:, 0:T], func=AF.Silu), reads=(gb,), writes=(gb,))
                    kb.op("dve", lambda e, g=g, v=v, jj=jj: e.tensor_tensor(out=self.act[:, jj, :], in0=g[:, 0:T], in1=v[:, 0:T], op=ALU.mult),
                          reads=(gb, vb), writes=(self.actb[jj],))
            act_list = [self.act[:, jj, :] for jj in range(43)]
            for mp in range(16):
                banks = [self.big_bank() for _ in range(2)]
                for (ka, kn) in ((0, 22), (22, 21)):
                    wv, wb = self.wload(Wd[(j0 + ka) * 128:(j0 + ka + kn) * 128, mp * 256:(mp + 1) * 256], kn, 256)
                    for mi in range(2):
                        bank, bb = banks[mi]
                        for k in range(kn):
                            kk = ka + k
                            self.mm(bank[:, 0:T], wv[:, k, mi * 128:(mi + 1) * 128], act_list[kk], kk == 0, kk == 42,
                                    reads=(wb, self.actb[kk]), writes=(bb,), inc=(k == kn - 1))
                for mi in range(2):
                    m = mp * 2 + mi
                    bank, bb = banks[mi]
                    kb.op("dve", lambda e, m=m, bank=bank: e.tensor_tensor(out=self.h[:, m, :], in0=bank[:, 0:T],
                                                                          in1=self.h[:, m, :], op=ALU.add),
                          reads=(bb, self.hb[m]), writes=(self.hb[m],))

    def ple(self, li, tile):
        kb = self.kb
        p0, npc, hs = tile
        self.barrier()
        for k in range(2):
            pt, ptb = self.scratch()
            kb.dma(pt[:, 0:npc * C], self.pin[li, k * 128:(k + 1) * 128, p0:p0 + npc * C], reads=(), writes=(ptb,))
            if hs:
                kb.dma(pt[:, npc * C:T], self.pin[li, k * 128:(k + 1) * 128, SEQ:SEQ + C], reads=(), writes=(ptb,))
            kb.op("dve", lambda e, k=k, pt=pt: e.tensor_copy(out=self.pe_bf[:, k, :], in_=pt[:, 0:T]), reads=(ptb,), writes=(self.pebf_b,))
        self.norm_to_xn(4 * 32 + li * 32)
        xn_list = [self.xn[:, c, :] for c in range(KC)]
        state = {}

        def handler(ci, mw, bank, bb):
            m = ci
            sg, sgb = self.scratch_long()
            kb.op("act", lambda e: e.activation(out=sg[:, 0:T], in_=bank[:, 0:T], func=AF.Sigmoid), reads=(bb,), writes=(sgb,))
            if m % 2 == 0:
                state["w"] = self.wload(self.w_pp[li][:, m * 128:(m + 2) * 128], 2, 256)
            wpv, wpb = state["w"]
            mo = (m % 2) * 128
            bank2, bb2 = self.big_bank()
            for k in range(2):
                self.mm(bank2[:, 0:T], wpv[:, k, mo:mo + 128], self.pe_bf[:, k, :], k == 0, k == 1,
                        reads=(wpb, self.pebf_b), writes=(bb2,), inc=(k == 1))
            kb.op("dve", lambda e: e.tensor_tensor(out=sg[:, 0:T], in0=bank2[:, 0:T], in1=sg[:, 0:T], op=ALU.mult),
                  reads=(bb2, sgb), writes=(sgb,))
            kb.op("dve", lambda e: e.tensor_tensor(out=self.h[:, m, :], in0=sg[:, 0:T], in1=self.h[:, m, :], op=ALU.add),
                  reads=(sgb, self.hb[m]), writes=(self.hb[m],))
        self.proj_fm(self.w_pg[li], 0, D, xn_list, self.xnb, handler)

    def out_proj(self, li):
        kb = self.kb
        mix_list = [self.mix[:, c, :] for c in range(KC)]

        def handler(ci, mw, bank, bb):
            m = ci
            kb.op("dve", lambda e: e.tensor_tensor(out=self.h[:, m, :], in0=bank[:, 0:T], in1=self.h[:, m, :], op=ALU.add),
                  reads=(bb, self.hb[m]), writes=(self.hb[m],))
        self.proj_fm(self.w_out[li], 0, D, mix_list, self.mixb, handler)

    def build(self):
        nc = self.nc
        kb = self.kb
        cfg = self.cfg
        self.declare_io()
        self.alloc()
        kb.dma(self.vecs_sb[:], self.vecs[:, :], writes=(self.vecs_b,))
        kb.dma(self.consts_sb[:], self.consts[:, :], writes=(self.consts_b,))
        kb.dma(self.cwf[:], self.cw_ffn.rearrange("l p n -> p l n"), writes=(self.cwf_b,))
        kb.dma(self.fhs[:], self.cf_in.rearrange("l p n -> p l n"), writes=tuple(self.fhsb))
        kb.dma(self.cwg[:], self.cw_gdn.rearrange("l p n -> p l n"), writes=(self.cwg_b,))
        kb.dma(self.ghs[:], self.cg_in.rearrange("l p n -> p l n"), writes=tuple(self.ghsb))
        self.eps_col = self.consts_sb[:, 256:257]
        self.one_col = self.consts_sb[:, 257:258]
        kb.op("dve", lambda e: e.tensor_copy(out=self.ident_bf[:], in_=self.consts_sb[:, 0:128]),
              reads=(self.consts_b,), writes=(self.cbf_b,))
        kb.op("dve", lambda e: e.tensor_copy(out=self.ones_bf[:], in_=self.consts_sb[:, 128:256]),
              reads=(self.consts_b,), writes=(self.cbf_b,))
        tiles = cfg["tiles"]
        for ti, tile in enumerate(tiles):
            p0, npc, hs = tile
            first_tile = (p0 == 0)
            last_tile = (p0 + npc * C == SEQ)
            kb.dma(self.h[:, :, 0:npc * C], self.xin[:, p0:p0 + npc * C].rearrange("(k p) t -> p k t", p=128),
                   reads=(), writes=tuple(self.hb))
            if hs:
                kb.dma(self.h[:, :, npc * C:T], self.xin[:, SEQ:SEQ + C].rearrange("(k p) t -> p k t", p=128),
                       reads=(), writes=tuple(self.hb))
            for li in range(DEPTH):
                if cfg["mixers"]:
                    self.norm_to_xn(li * 32)
                    self.mixers(li, tile, first_tile, last_tile)
                    self.out_proj(li)
                if cfg["ffn"]:
                    self.norm_to_xn(2 * 32 + li * 32)
                    self.ffn(li, tile, first_tile)
                if cfg["ple"]:
                    self.ple(li, tile)

            def yfn(c, wcol):
                ys, ysb = self.scratch()
                kb.op("dve", lambda e: e.scalar_tensor_tensor(out=ys[:, 0:T], in0=self.h[:, c, :], scalar=wcol,
                                                              in1=self.rstd[:], op0=ALU.mult, op1=ALU.mult),
                      reads=(self.hb[c], self.rstd_b, self.vecs_b), writes=(ysb,))
                kb.dma(self.yout[c * 128:(c + 1) * 128, p0:p0 + npc * C], ys[:, 0:npc * C],
                       reads=(ysb,), is_output=True)
                if hs:
                    kb.dma(self.yout[c * 128:(c + 1) * 128, SEQ:SEQ + C], ys[:, npc * C:T],
                           reads=(ysb,), is_output=True)
            self.norm(6 * 32, yfn)
            if last_tile:
                for li in range(DEPTH):
                    kb.dma(self.o_cf[0, li], self.fh[:, li, :], reads=(self.fhb[li],), is_output=True)
                    kb.dma(self.o_cg[0, li], self.gh[:, li, :], reads=(self.ghb[li],), is_output=True)
            if hs:
                for li in range(DEPTH):
                    kb.dma(self.o_cf[1, li], self.fhs[:, li, :], reads=(self.fhsb[li],), is_output=True)
                    kb.dma(self.o_cg[1, li], self.ghs[:, li, :], reads=(self.ghsb[li],), is_output=True)
        kb.finish()

    def tf32(self, off, n):
        return self.hflat[:, off:off + n]

    def tbf(self, off, n_bf):
        return self.hflat[:, off:off + n_bf // 2].bitcast(BF16)

    def dump(self, i, ap, rows=128, cols=T):
        if self.dbg is None:
            return
        kb = self.kb
        tmp, tmpb = self.scratch_long()
        kb.op("dve", lambda e: e.tensor_copy(out=tmp[0:rows, 0:cols], in_=ap), reads=tuple(self._dump_reads), writes=(tmpb,))
        kb.dma(self.dbg[i, 0:rows, 0:cols], tmp[0:rows, 0:cols], reads=(tmpb,), is_output=True)

    def barrier(self):
        kb = self.kb
        toks = [Tok(kb.E[n].sem, kb.E[n].key, kb.E[n].count) for n in ("pe", "act", "dve") if kb.E[n].count > 0]
        toks += kb.all_dma
        kb.all_dma = []
        for n in ("pe", "act", "dve", "sp"):
            kb._wait(kb.E[n], [t for t in toks if t.key != kb.E[n].key])

    def mixers(self, li, tile, first_tile, last_tile):
        kb = self.kb
        p0, npc, hs = tile
        W = self.w_in[li]
        kb.dma(self.hspill[:, :], self.hflat, reads=tuple(self.hb), writes=())
        self.barrier()
        tb = self.buf("tmp")
        self.S_all = self.arena[:, KC * T:KC * T + 8192].bitcast(F32)
        Sb = self.buf("S")
        if first_tile:
            kb.op("dve", lambda e: e.memset(self.S_all, 0.0), writes=(Sb,))
        else:
            kb.dma(self.S_all, self.st_scr[li], writes=(Sb,))
        self.Sb = Sb
        sp = self.tf32(12800, 64)
        spb = self.buf("sp")
        kb.dma(sp, self.smallp[:, :], writes=(spb,))
        hl = self.tf32(12864, 16)
        kb.dma(hl, self.hlb[:, :], writes=(spb,))
        negb = self.tf32(12880, 8)
        kb.op("dve", lambda e: e.tensor_scalar(out=negb, in0=sp[:, 0:8], scalar1=-1.0, scalar2=None, op0=ALU.mult),
              reads=(spb,), writes=(spb,))
        e01 = self.tf32(12888, 16)
        kb.op("act", lambda e: e.activation(out=e01, in_=hl, func=AF.Exp), reads=(spb,), writes=(spb,))
        den = self.tf32(12904, 8)
        kb.op("dve", lambda e: e.tensor_tensor(out=den, in0=e01[:, 0:8], in1=e01[:, 8:16], op=ALU.add), reads=(spb,), writes=(spb,))
        kb.op("dve", lambda e: e.reciprocal(out=den, in_=den), reads=(spb,), writes=(spb,))
        sm = self.tf32(12912, 16)
        kb.op("dve", lambda e: e.tensor_tensor(out=sm[:, 0:8], in0=e01[:, 0:8], in1=den, op=ALU.mult), reads=(spb,), writes=(spb,))
        kb.op("dve", lambda e: e.tensor_tensor(out=sm[:, 8:16], in0=e01[:, 8:16], in1=den, op=ALU.mult), reads=(spb,), writes=(spb,))
        lbv = self.tf32(12928, 8)
        if li == 0:
            kb.op("dve", lambda e: e.tensor_tensor(out=lbv, in0=sm[:, 0:8], in1=sm[:, 0:8], op=ALU.subtract), reads=(spb,), writes=(spb,))
        else:
            kb.op("dve", lambda e: e.tensor_tensor(out=lbv, in0=sm[:, 0:8], in1=sm[:, 8:16], op=ALU.add), reads=(spb,), writes=(spb,))
            kb.op("dve", lambda e: e.tensor_tensor(out=lbv, in0=lbv, in1=sm[:, 0:8], op=ALU.subtract), reads=(spb,), writes=(spb,))
        oml = self.tf32(12936, 8)
        kb.op("dve", lambda e: e.tensor_scalar(out=oml, in0=lbv, scalar1=-1.0, scalar2=1.0, op0=ALU.mult, op1=ALU.add),
              reads=(spb,), writes=(spb,))
        noml = self.tf32(12944, 8)
        kb.op("dve", lambda e: e.tensor_scalar(out=noml, in0=oml, scalar1=-1.0, scalar2=None, op0=ALU.mult), reads=(spb,), writes=(spb,))
        self.sp, self.spb, self.negb, self.lbv, self.oml, self.noml = sp, spb, negb, lbv, oml, noml
        xn_list = [self.xn[:, c, :] for c in range(KC)]
        self.lr_bf = self.tbf(128, 416)
        self.lrb = self.buf("lr")
        self.wgg_bf = self.tbf(336, 512)
        wtmp = self.tf32(12288, 512)
        kb.dma(wtmp[0:16, :], self.w_gg[li], writes=(self.lrb,))
        kb.op("dve", lambda e: e.tensor_copy(out=self.wgg_bf[0:16, :], in_=wtmp[0:16, :]), reads=(self.lrb,), writes=(self.lrb,))

        def lr_h(ci, mw, bank, bb):
            kb.op("act", lambda e: e.activation(out=self.lr_bf[0:16, :], in_=bank[0:16, 0:T], func=AF.Copy),
                  reads=(bb,), writes=(self.lrb,))
        self.proj_fm(W, O_GLR, 16, xn_list, self.xnb, lr_h)
        for hd in range(4):
            self.gla_like("gla", li, hd, tile, first_tile, last_tile)
        for hd in range(8):
            self.gla_like("hg", li, hd, tile, first_tile, last_tile)
        self.barrier()
        if self.cfg.get("gdn", True):
            self.gdn(li, tile, first_tile, last_tile)
        else:
            for c in range(8, 24):
                kb.op("dve", lambda e, c=c: e.memset(self.mix[:, c, :], 0.0), writes=(self.mixb[c],))
        if not last_tile and not hs:
            kb.dma(self.st_scr[li], self.S_all, reads=(Sb,))
        self.barrier()
        E = kb.E["sp"]
        kb._wait(E, [Tok(kb.E[n].sem, kb.E[n].key, kb.E[n].count) for n in ("pe", "act", "dve")])
        kb.dma(self.hflat, self.hspill[:, :], writes=tuple(self.hb))


    def gdn(self, li, tile, first_tile, last_tile):
        kb = self.kb
        p0, npc, hs = tile
        nch = npc + (1 if hs else 0)
        W = self.w_in[li]
        xn_list = [self.xn[:, c, :] for c in range(KC)]
        C_ = self.consts_sb
        I32, ones32, onesw = C_[0:32, 0:32], C_[0:32, 128:160], C_[0:32, 128:256]
        Tri, SU, neg32 = C_[0:32, 1344:1376], C_[0:32, 1376:1408], C_[0:32, 1408:1440]
        M1, M2, M3 = C_[0:32, 1440:1504], C_[0:32, 1504:1568], C_[0:32, 1568:1632]
        I2, Tri2 = C_[0:32, 1632:1696], C_[0:32, 1696:1760]
        cb = self.consts_b
        NG = nch * 16
        raw = self.tf32(128, 416)[0:32, :]
        g_all = self.tf32(544, 208)[0:32, :]
        lb_all = self.tf32(752, 208)[0:32, :]
        beta = self.tf32(960, 208)[0:32, :]
        bcs = self.tf32(1168, 208)[0:32, :]
        beb = self.tf32(1376, 208)[0:32, :]
        dd = self.tf32(1584, 208)[0:32, :]
        tA = self.tf32(1792, 208)[0:32, :]
        expA = self.tf32(2000, 208)[0:32, :]
        gb = self.buf("gates")
        wv, wb = self.wload(W[:, O_DB:O_DB + 32], KC, 32)
        bank, bb = self.big_bank()
        for n in range(nch):
            for k in range(KC):
                self.mm(bank[0:32, n * 32:(n + 1) * 32], self.xn[:, k, n * 32:(n + 1) * 32], wv[:, k, :], k == 0, k == KC - 1,
                        reads=(wb, self.xnb[k]), writes=(bb,), inc=(k == KC - 1))
        kb.op("act", lambda e: e.activation(out=raw[:, 0:nch * 32], in_=bank[0:32, 0:nch * 32], func=AF.Copy), reads=(bb,), writes=(gb,))
        raw3 = raw[:, 0:nch * 32].rearrange("p (n c) -> p n c", n=nch)
        v3 = lambda a: a[:, 0:NG].rearrange("p (n h) -> p n h", n=nch)
        grow = self.tf32(12288, 416)[0:32, :]
        kb.dma(grow, self.gdnrow[0:32, li * 416:(li + 1) * 416], writes=(self.grow_b,))
        alog = grow[:, 0:NG]
        dtb = grow[:, 208:208 + NG]
        kb.op("act", lambda e: e.activation(out=v3(beta), in_=raw3[:, :, 0:16], func=AF.Sigmoid), reads=(gb,), writes=(gb,))
        kb.op("act", lambda e: e.activation(out=lb_all[:, 0:NG], in_=beta[:, 0:NG], func=AF.Ln), reads=(gb,), writes=(gb,))
        kb.op("dve", lambda e: e.tensor_tensor(out=v3(tA), in0=raw3[:, :, 16:32], in1=v3(dtb), op=ALU.add), reads=(gb, self.grow_b), writes=(gb,))
        kb.op("act", lambda e: e.activation(out=tA[:, 0:NG], in_=tA[:, 0:NG], func=AF.Exp), reads=(gb,), writes=(gb,))
        kb.op("act", lambda e: e.activation(out=tA[:, 0:NG], in_=tA[:, 0:NG], func=AF.Ln, bias=self.one_col[0:32, :]), reads=(gb, cb), writes=(gb,))
        kb.op("act", lambda e: e.activation(out=expA[:, 0:NG], in_=alog, func=AF.Exp), reads=(self.grow_b,), writes=(gb,))
        kb.op("dve", lambda e: e.scalar_tensor_tensor(out=g_all[:, 0:NG], in0=tA[:, 0:NG], scalar=-1.0, in1=expA[:, 0:NG],
                                                      op0=ALU.mult, op1=ALU.mult), reads=(gb,), writes=(gb,))
        b6, b6b = self.pbank[6], self.pbb[6]
        b7, b7b = self.pbank[7], self.pbb[7]
        self.mm(b6[0:32, 0:NG], Tri, g_all[:, 0:NG], True, True, reads=(cb, gb), writes=(b6b,), inc=True)
        kb.op("dve", lambda e: e.tensor_tensor(out=beb[:, 0:NG], in0=b6[0:32, 0:NG], in1=lb_all[:, 0:NG], op=ALU.add), reads=(b6b, gb), writes=(gb,))
        kb.op("act", lambda e: e.activation(out=beb[:, 0:NG], in_=beb[:, 0:NG], func=AF.Exp), reads=(gb,), writes=(gb,))
        self.mm(b7[0:32, 0:NG], SU, g_all[:, 0:NG], True, True, reads=(cb, gb), writes=(b7b,), inc=True)
        kb.op("act", lambda e: e.activation(out=dd[:, 0:NG], in_=b7[0:32, 0:NG], func=AF.Exp), reads=(b7b,), writes=(gb,))
        for grp in range(8):
            self.gdn_group(li, grp, tile, first_tile, last_tile, g_all, lb_all, beta, beb, dd, gb)

    def gdn_group(self, li, grp, tile, first_tile, last_tile, g_all, lb_all, beta, beb, dd, gb):
        kb = self.kb
        p0, npc, hs = tile
        nch = npc + (1 if hs else 0)
        W = self.w_in[li]
        xn_list = [self.xn[:, c, :] for c in range(KC)]
        C_ = self.consts_sb
        I32, ones32, onesw = C_[0:32, 0:32], C_[0:32, 128:160], C_[0:32, 128:256]
        Tri, SU, neg32 = C_[0:32, 1344:1376], C_[0:32, 1376:1408], C_[0:32, 1408:1440]
        M1, M2, M3 = C_[0:32, 1440:1504], C_[0:32, 1504:1568], C_[0:32, 1568:1632]
        I2, Tri2 = C_[0:32, 1632:1696], C_[0:32, 1696:1760]
        cb = self.consts_b
        h0 = grp * 2
        S_bf = self.tbf(0, 256)
        S_bfb = [self.buf("Sbf") for _ in range(2)]
        Sb = self.Sb
        qn = self.tbf(2208, 832).rearrange("p (h t) -> p h t", h=2)
        kn = self.tbf(2624, 832).rearrange("p (h t) -> p h t", h=2)
        vn = self.tbf(3040, 832).rearrange("p (h t) -> p h t", h=2)
        zg = self.tbf(3456, 832).rearrange("p (h t) -> p h t", h=2)
        qe = self.tbf(3872, 832).rearrange("p (h t) -> p h t", h=2)
        nwT = self.tbf(4288, 832).rearrange("p (h t) -> p h t", h=2)
        EB = self.tf32(4704, 832).rearrange("p (h t) -> p h t", h=2)
        k_tok = self.tbf(5536, 13 * 256)[0:32, :].rearrange("p (n h d) -> p n h d", n=13, h=2)
        v_tok = self.tbf(7200, 13 * 256)[0:32, :].rearrange("p (n h d) -> p n h d", n=13, h=2)
        Gtri = self.tf32(8864, 832)[0:32, :].rearrange("p (n h j) -> p n h j", n=13, h=2)
        PT = self.tbf(9696, 832)[0:32, :].rearrange("p (n h j) -> p n h j", n=13, h=2)
        TB = self.tbf(10112, 832)[0:32, :].rearrange("p (n h j) -> p n h j", n=13, h=2)
        TW = self.tbf(10528, 832)[0:32, :].rearrange("p (n h j) -> p n h j", n=13, h=2)
        vnw = self.tbf(12952, 512)[0:32, :].rearrange("p (a h d) -> p a h d", a=2, h=2)
        qnb, knb, vnb, zgb, qeb, nwTb, EBb, ktkb, vtkb, Gtb, PTb, TBb, TWb = [self.buf("g") for _ in range(13)]
        vnwb = [self.buf("vnw") for _ in range(2)]
        g3 = lambda a: a[:, 0:nch * 16].rearrange("p (n h) -> p n h", n=nch)
        def mk(which):
            def handler(ci, mw, bank, bb):
                hd = h0 + ci
                chunk = {"q": 0, "k": 16, "v": 32}[which] + hd
                cy, cyb = self.conv_chunk(tile, bank, bb, 4, self.cwg[:, li, :], self.cwg_b, chunk * 4,
                                          self.gh[:, li, :], self.ghb[li], self.ghs[:, li, :], self.ghsb[li], chunk * 3, first_tile)
                kb.op("act", lambda e: e.activation(out=cy[:, 0:T], in_=cy[:, 0:T], func=AF.Silu), reads=(cyb,), writes=(cyb,))
                if which == "v":
                    kb.op("dve", lambda e: e.tensor_copy(out=vn[:, ci, :], in_=cy[:, 0:T]), reads=(cyb,), writes=(vnb,))
                    return
                sq, sqb = self.sq[ci % 2], self.sqb[ci % 2]
                kb.op("act", lambda e: e.activation(out=sq[:], in_=cy[:, 0:T], func=AF.Square), reads=(cyb,), writes=(sqb,))
                ssb, ssbb = self.big_bank()
                self.mm(ssb[:, 0:T], self.ones_bf[:], sq[:], True, True, reads=(sqb, self.cbf_b), writes=(ssbb,), inc=True)
                rs, rsb = self.scratch()
                kb.op("act", lambda e: e.activation(out=rs[:, 0:T], in_=ssb[:, 0:T], func=AF.Sqrt, bias=self.eps_col),
                      reads=(ssbb, cb), writes=(rsb,))
                kb.op("dve", lambda e: e.reciprocal(out=rs[:, 0:T], in_=rs[:, 0:T]), reads=(rsb,), writes=(rsb,))
                if which == "q":
                    kb.op("dve", lambda e: e.scalar_tensor_tensor(out=qn[:, ci, :], in0=cy[:, 0:T], scalar=128.0 ** -0.5, in1=rs[:, 0:T],
                                                                  op0=ALU.mult, op1=ALU.mult), reads=(cyb, rsb), writes=(qnb,))
                else:
                    kb.op("dve", lambda e: e.tensor_tensor(out=kn[:, ci, :], in0=cy[:, 0:T], in1=rs[:, 0:T], op=ALU.mult),
                          reads=(cyb, rsb), writes=(knb,))
            return handler
        self.proj_fm(W, O_DQ + h0 * 128, 256, xn_list, self.xnb, mk("q"))
        self.proj_fm(W, O_DK + h0 * 128, 256, xn_list, self.xnb, mk("k"))
        self.proj_fm(W, O_DV + h0 * 128, 256, xn_list, self.xnb, mk("v"))

        def z_h(ci, mw, bank, bb):
            kb.op("act", lambda e: e.activation(out=zg[:, ci, :], in_=bank[:, 0:T], func=AF.Silu), reads=(bb,), writes=(zgb,))
        self.proj_fm(W, O_DZ + h0 * 128, 256, xn_list, self.xnb, z_h)
        for (src, srcb, dst, dstb) in ((kn, knb, k_tok, ktkb), (vn, vnb, v_tok, vtkb)):
            for n0 in range(0, nch, 2):
                nn = min(2, nch - n0)
                bank, bb = self.misc_bank()
                for j in range(nn):
                    for hh in range(2):
                        c0 = (j * 2 + hh) * 128
                        self.mm(bank[0:32, c0:c0 + 128], src[:, hh, (n0 + j) * 32:(n0 + j + 1) * 32], self.ident_bf[:], True, True,
                                reads=(srcb, self.cbf_b), writes=(bb,), inc=(j == nn - 1 and hh == 1))
                kb.op("act", lambda e, bank=bank, n0=n0, nn=nn, dst=dst: e.activation(
                    out=dst[:, n0:n0 + nn, :, :], in_=bank[0:32, 0:nn * 256].rearrange("p (n h d) -> p n h d", n=nn, h=2), func=AF.Copy),
                    reads=(bb,), writes=(dstb,))
        for n in range(nch):
            kb.op("dve", lambda e, n=n: e.tensor_tensor(
                out=Gtri[:, n, :, :], in0=Tri2.rearrange("p (h j) -> p h j", h=2),
                in1=g3(g_all)[:, n, h0:h0 + 2].unsqueeze(2).broadcast_to([32, 2, 32]), op=ALU.mult),
                reads=(gb, cb), writes=(Gtb,))
        for hh in range(2):
            bank, bb = self.misc_bank()
            for n in range(nch):
                self.mm(bank[:, n * 32:(n + 1) * 32], onesw, Gtri[:, n, hh, :], True, True,
                        reads=(cb, Gtb), writes=(bb,), inc=(n == nch - 1))
            kb.op("act", lambda e, bank=bank, hh=hh: e.activation(out=EB[:, hh, 0:nch * 32], in_=bank[:, 0:nch * 32], func=AF.Exp),
                  reads=(bb,), writes=(EBb,))
            kb.op("dve", lambda e, hh=hh: e.tensor_tensor(out=qe[:, hh, 0:nch * 32], in0=qn[:, hh, 0:nch * 32], in1=EB[:, hh, 0:nch * 32], op=ALU.mult),
                  reads=(qnb, EBb), writes=(qeb,))
        def chunk_steps(n, sl):
            base = 10944 + sl * 320
            PQ = [self.tf32(base, 128)[0:32, :], self.tf32(base + 128, 128)[0:32, :]]
            Rm = self.tf32(base + 256, 64)[0:32, :]
            Rt = PQ[1][:, 0:64]
            Ebuf = PQ[1][:, 64:128]
            slb = self.slotb[sl]
            bA, bAb = self.pbank[sl], self.pbb[sl]
            bB, bBb = self.pbank[4 + sl], self.pbb[4 + sl]
            cols = slice(n * 32, (n + 1) * 32)
            Gn = Gtri[:, n, :, :].rearrange("p h j -> p (h j)")
            kb.op("dve", lambda e: e.tensor_tensor(
                out=Rt.rearrange("p (h j) -> p h j", h=2), in0=I2.rearrange("p (h j) -> p h j", h=2),
                in1=g3(lb_all)[:, n, h0:h0 + 2].unsqueeze(2).broadcast_to([32, 2, 32]), op=ALU.mult), reads=(gb, cb), writes=(slb,))
            kb.op("dve", lambda e: e.tensor_tensor(out=Rt, in0=Rt, in1=Gn, op=ALU.add), reads=(slb, Gtb), writes=(slb,))
            bkk, bkkb = bA, bAb
            for hh in range(2):
                self.mm(bkk[0:32, hh * 32:(hh + 1) * 32], kn[:, hh, cols], kn[:, hh, cols], True, True,
                        reads=(knb,), writes=(bkkb,), inc=False)
                self.mm(bkk[0:32, 64 + hh * 32:64 + (hh + 1) * 32], kn[:, hh, cols], qn[:, hh, cols], True, True,
                        reads=(knb, qnb), writes=(bkkb,), inc=(hh == 1))
            yield

            def expo(first_all, per_head, mask, dst_fn):
                bE, bEb = bB, bBb
                lhsT_a, rhs_a = first_all
                self.mm(bE[0:32, 0:64], lhsT_a, rhs_a, True, False, reads=(slb, Gtb, cb), writes=(bEb,), inc=False)
                for hh in range(2):
                    lh, rh = per_head(hh)
                    self.mm(bE[0:32, hh * 32:(hh + 1) * 32], lh, rh, False, False, reads=(slb, Gtb, cb), writes=(bEb,), inc=False)
                self.mm(bE[0:32, 0:64], I32, mask, False, True, reads=(cb,), writes=(bEb,), inc=True)
                yield
                kb.op("act", lambda e: e.activation(out=Ebuf, in_=bE[0:32, 0:64], func=AF.Exp), reads=(bEb,), writes=(slb,))
                yield
                dst_fn()
            yield from expo((ones32, Rt), lambda hh: (Gtri[:, n, hh, :], neg32), M1,
                            lambda: kb.op("dve", lambda e: e.scalar_tensor_tensor(out=PQ[0][:, 0:64], in0=bkk[0:32, 0:64], scalar=-1.0, in1=Ebuf,
                                                                                  op0=ALU.mult, op1=ALU.mult), reads=(bkkb, slb), writes=(slb,)))
            yield
            yield from expo((neg32, Gn), lambda hh: (Rt[:, hh * 32:(hh + 1) * 32], ones32), M2,
                            lambda: kb.op("dve", lambda e: e.scalar_tensor_tensor(out=PQ[0][:, 64:128], in0=bkk[0:32, 0:64], scalar=-1.0, in1=Ebuf,
                                                                                  op0=ALU.mult, op1=ALU.mult), reads=(bkkb, slb), writes=(slb,)))
            yield
            yield from expo((ones32, Gn), lambda hh: (Gtri[:, n, hh, :], neg32), M3,
                            lambda: kb.op("dve", lambda e: e.tensor_tensor(out=PT[:, n, :, :].rearrange("p h j -> p (h j)"), in0=bkk[0:32, 64:128], in1=Ebuf,
                                                                           op=ALU.mult), reads=(bkkb, slb), writes=(PTb,)))
            kb.op("dve", lambda e: e.tensor_tensor(out=Rm, in0=PQ[0][:, 0:64], in1=I2, op=ALU.add), reads=(slb, cb), writes=(slb,))
            yield
            cur = 0
            for lev in range(1, 5):
                nxt = 1 - cur
                bp, bpb = bA, bAb
                for hh in range(2):
                    Pp = PQ[cur][:, hh * 32:(hh + 1) * 32]
                    Qp = PQ[cur][:, 64 + hh * 32:64 + (hh + 1) * 32]
                    if lev < 4:
                        self.mm(bp[0:32, hh * 32:(hh + 1) * 32], Qp, Pp, True, True, reads=(slb,), writes=(bpb,), inc=False)
                    self.mm(bp[0:32, 64 + hh * 32:64 + (hh + 1) * 32], Pp, Qp, True, True, reads=(slb,), writes=(bpb,), inc=(hh == 1))
                yield
                lo = 0 if lev < 4 else 64
                kb.op("act", lambda e, nxt=nxt, lo=lo, bp=bp: e.activation(out=PQ[nxt][:, lo:128], in_=bp[0:32, lo:128], func=AF.Copy),
                      reads=(bpb,), writes=(slb,))
                yield
                br, brb = bB, bBb
                for hh in range(2):
                    Qn_ = PQ[nxt][:, 64 + hh * 32:64 + (hh + 1) * 32]
                    self.mm(br[0:32, hh * 32:(hh + 1) * 32], Qn_, Rm[:, hh * 32:(hh + 1) * 32], True, True, reads=(slb,), writes=(brb,), inc=(hh == 1))
                yield
                kb.op("dve", lambda e, br=br: e.tensor_tensor(out=Rm, in0=br[0:32, 0:64], in1=Rm, op=ALU.add), reads=(brb, slb), writes=(slb,))
                yield
                cur = nxt
            for hh in range(2):
                col = n * 16 + h0 + hh
                kb.op("dve", lambda e, hh=hh, col=col: e.tensor_scalar(out=TB[:, n, hh, :], in0=Rm[:, hh * 32:(hh + 1) * 32],
                                                                       scalar1=beta[:, col:col + 1], scalar2=None, op0=ALU.mult),
                      reads=(slb, gb), writes=(TBb,))
                kb.op("dve", lambda e, hh=hh, col=col: e.tensor_scalar(out=TW[:, n, hh, :], in0=Rm[:, hh * 32:(hh + 1) * 32],
                                                                       scalar1=beb[:, col:col + 1], scalar2=None, op0=ALU.mult),
                      reads=(slb, gb), writes=(TWb,))

        NSL = 4
        for n0 in range(0, nch, NSL):
            alive = [chunk_steps(n, n - n0) for n in range(n0, min(n0 + NSL, nch))]
            while alive:
                still = []
                for g_ in alive:
                    try:
                        next(g_)
                        still.append(g_)
                    except StopIteration:
                        pass
                alive = still
        for hh in range(2):
            bank, bb = self.misc_bank()
            for n in range(nch):
                self.mm(bank[:, n * 32:(n + 1) * 32], k_tok[:, n, hh, :], TW[:, n, hh, :], True, True,
                        reads=(ktkb, TWb), writes=(bb,), inc=(n == nch - 1))
            kb.op("act", lambda e, bank=bank, hh=hh: e.activation(out=nwT[:, hh, 0:nch * 32], in_=bank[:, 0:nch * 32], func=AF.Copy, scale=-1.0),
                  reads=(bb,), writes=(nwTb,))
        ob = [(self.pbank[4], self.pbb[4]), (self.pbank[5], self.pbb[5])]
        b6, b6b = self.pbank[6], self.pbb[6]
        b7, b7b = self.pbank[7], self.pbb[7]
        Sv = [self.S_all[:, 1024 + (h0 + hh) * 128:1024 + (h0 + hh + 1) * 128] for hh in range(2)]
        for hh in range(2):
            kb.op("act", lambda e, hh=hh: e.activation(out=S_bf[:, hh * 128:(hh + 1) * 128], in_=Sv[hh], func=AF.Copy),
                  reads=(Sb,), writes=(S_bfb[hh],))
        for n in range(nch):
            cols = slice(n * 32, (n + 1) * 32)
            for hh in range(2):
                hd = h0 + hh
                if hs and n == npc:
                    kb.dma(self.o_st_gdn[0, li, hd], Sv[hh], reads=(Sb,), is_output=True)
                    kb.dma(Sv[hh], self.st_gdn[li, hd], writes=(Sb,))
                    kb.op("act", lambda e, hh=hh: e.activation(out=S_bf[:, hh * 128:(hh + 1) * 128], in_=Sv[hh], func=AF.Copy),
                          reads=(Sb,), writes=(S_bfb[hh],))
                Sbf = S_bf[:, hh * 128:(hh + 1) * 128]
                bv, bvb = (b6, b6b) if hh == 0 else (b7, b7b)
                self.mm(bv[0:32, 0:128], TB[:, n, hh, :], v_tok[:, n, hh, :], True, False, reads=(TBb, vtkb), writes=(bvb,), inc=False)
                self.mm(bv[0:32, 0:128], nwT[:, hh, cols], Sbf, False, True, reads=(nwTb, S_bfb[hh]), writes=(bvb,), inc=True)
                kb.op("act", lambda e, hh=hh, bv=bv: e.activation(out=vnw[:, 0, hh, :], in_=bv[0:32, 0:128], func=AF.Copy),
                      reads=(bvb,), writes=(vnwb[hh],))
                col = n * 16 + hd
                kb.op("dve", lambda e, hh=hh, bv=bv, col=col: e.tensor_scalar(out=vnw[:, 1, hh, :], in0=vnw[:, 0, hh, :],
                                                                              scalar1=dd[:, col:col + 1], scalar2=None, op0=ALU.mult),
                      reads=(vnwb[hh], gb), writes=(vnwb[hh],))
                bo, bob = ob[hh]
                self.mm(bo[:, cols], vnw[:, 0, hh, :], PT[:, n, hh, :], True, False, reads=(vnwb[hh], PTb), writes=(bob,), inc=False)
                self.mm(bo[:, cols], Sbf, qe[:, hh, cols], False, True, reads=(S_bfb[hh], qeb), writes=(bob,), inc=True)
                self.mm(bv[:, 128:256], k_tok[:, n, hh, :], vnw[:, 1, hh, :], True, True, reads=(ktkb, vnwb[hh]), writes=(bvb,), inc=True)
                kb.op("dve", lambda e, hh=hh, bv=bv, n=n: e.scalar_tensor_tensor(
                    out=Sv[hh], in0=Sv[hh], scalar=EB[:, hh, n * 32 + 31:n * 32 + 32], in1=bv[:, 128:256], op0=ALU.mult, op1=ALU.add),
                    reads=(Sb, EBb, bvb), writes=(Sb,))
                kb.op("act", lambda e, hh=hh: e.activation(out=S_bf[:, hh * 128:(hh + 1) * 128], in_=Sv[hh], func=AF.Copy),
                      reads=(Sb,), writes=(S_bfb[hh],))
        for hh in range(2):
            hd = h0 + hh
            if hs:
                kb.dma(self.o_st_gdn[1, li, hd], Sv[hh], reads=(Sb,), is_output=True)
            elif last_tile:
                kb.dma(self.o_st_gdn[0, li, hd], Sv[hh], reads=(Sb,), is_output=True)
        for hh in range(2):
            bo, bob = ob[hh]
            sq, sqb = self.sq[hh], self.sqb[hh]
            kb.op("act", lambda e, bo=bo, sq=sq: e.activation(out=sq[:], in_=bo[:, 0:T], func=AF.Square), reads=(bob,), writes=(sqb,))
            ssb, ssbb = self.big_bank()
            self.mm(ssb[:, 0:T], self.ones_bf[:], sq[:], True, True, reads=(sqb, self.cbf_b), writes=(ssbb,), inc=True)
            rs, rsb = self.scratch()
            kb.op("act", lambda e, rs=rs, ssb=ssb: e.activation(out=rs[:, 0:T], in_=ssb[:, 0:T], func=AF.Sqrt, scale=1.0 / 128, bias=self.eps_col),
                  reads=(ssbb, cb), writes=(rsb,))
            kb.op("dve", lambda e, rs=rs: e.reciprocal(out=rs[:, 0:T], in_=rs[:, 0:T]), reads=(rsb,), writes=(rsb,))
            tmp, tmpb = self.scratch()
            kb.op("dve", lambda e, bo=bo, tmp=tmp, rs=rs: e.scalar_tensor_tensor(
                out=tmp[:, 0:T], in0=bo[:, 0:T], scalar=self.sp[:, 12 + li:13 + li], in1=rs[:, 0:T], op0=ALU.mult, op1=ALU.mult),
                reads=(bob, rsb, self.spb), writes=(tmpb,))
            kb.op("dve", lambda e, tmp=tmp, hh=hh: e.tensor_tensor(out=self.mix[:, 8 + h0 + hh, :], in0=tmp[:, 0:T], in1=zg[:, hh, :], op=ALU.mult),
                  reads=(tmpb, zgb), writes=(self.mixb[8 + h0 + hh],))

    def gla_like(self, kind, li, hd, tile, first_tile, last_tile):
        kb = self.kb
        p0, npc, hs = tile
        nch = npc + (1 if hs else 0)
        W = self.w_in[li]
        xn_list = [self.xn[:, c, :] for c in range(KC)]
        if kind == "gla":
            ndv, oq, ok, ov, og = 2, O_GQ + hd * 128, O_GK + hd * 128, O_GV + hd * 256, O_GG + hd * 256
            soff, mix0, ncol = hd * 256, hd * 2, 8 + li * 2
            st_in, st_out = self.st_gla, self.o_st_gla
        else:
            ndv, oq, of_, ov, og = 1, O_HQ + hd * 128, O_HF + hd * 128, O_HI + hd * 128, O_HG + hd * 128
            soff, mix0, ncol = 3072 + hd * 128, 24 + hd, 14 + li
            st_in, st_out = self.st_hg, self.o_st_hg
        dv = 128 * ndv
        S = self.S_all[:, soff:soff + dv]
        Sb = self.Sb
        S_bf = self.tbf(0, 256)
        S_bfb = self.buf("Sbf")
        eb = self.tf32(592, T)
        enb = self.tf32(1008, T)
        cs = self.tf32(1424, T)
        fz = self.tf32(6208, T)
        qt = self.tbf(1840, T)
        kt = self.tbf(2048, T)
        sg = self.tbf(2256, 2 * T).rearrange("p (e t) -> p e t", e=2)
        v_tok = self.tbf(2672, 13 * 256).rearrange("p (n d) -> p n d", n=13)
        k_tok = self.tbf(4336, 13 * 128).rearrange("p (n d) -> p n d", n=13)
        AT = self.tbf(5168, T)
        ebb, csb, fzb, qtb, ktb, sgb, vtb, ktkb, ATb = [self.buf("t") for _ in range(9)]
        maskT = self.consts_sb[0:32, 512:512 + T]
        rmask = self.consts_sb[:, 928:928 + T]
        b4, b4b = self.pbank[4], self.pbb[4]
        b5, b5b = self.pbank[5], self.pbb[5]
        b6, b6b = self.pbank[6], self.pbb[6]
        b7, b7b = self.pbank[7], self.pbb[7]
        ob = [(b4, b4b), (b5, b5b)]
        if kind == "gla":
            self.mm(b6[:, 0:T], self.wgg_bf[0:16, hd * 128:(hd + 1) * 128], self.lr_bf[0:16, :], True, True,
                    reads=(self.lrb,), writes=(b6b,), inc=True)
            kb.op("act", lambda e: e.activation(out=fz, in_=b6[:, 0:T], func=AF.Exp, scale=-1.0,
                                                bias=self.negb[:, li * 4 + hd:li * 4 + hd + 1]),
                  reads=(b6b, self.spb), writes=(fzb,))
            kb.op("act", lambda e: e.activation(out=fz, in_=fz, func=AF.Ln, bias=self.one_col), reads=(fzb, self.consts_b), writes=(fzb,))
            kb.op("dve", lambda e: e.tensor_tensor_scan(out=cs, data0=rmask, data1=fz, initial=0.0, op0=ALU.mult, op1=ALU.add),
                  reads=(fzb, self.consts_b), writes=(csb,))
            kb.op("act", lambda e: e.activation(out=eb, in_=cs, func=AF.Exp, scale=-1.0 / 16.0), reads=(csb,), writes=(ebb,))
            kb.op("act", lambda e: e.activation(out=enb, in_=cs, func=AF.Exp, scale=1.0 / 16.0), reads=(csb,), writes=(ebb,))
        else:
            def f_h(ci, mw, bank, bb):
                kb.op("act", lambda e: e.activation(out=fz, in_=bank[:, 0:T], func=AF.Sigmoid), reads=(bb,), writes=(fzb,))
            self.proj_fm(W, of_, 128, xn_list, self.xnb, f_h)
            kb.op("dve", lambda e: e.tensor_scalar(out=eb, in0=fz, scalar1=self.oml[:, hd:hd + 1], scalar2=self.lbv[:, hd:hd + 1],
                                                   op0=ALU.mult, op1=ALU.add), reads=(fzb, self.spb), writes=(ebb,))
            kb.op("act", lambda e: e.activation(out=eb, in_=eb, func=AF.Ln), reads=(ebb,), writes=(ebb,))
            kb.op("dve", lambda e: e.tensor_tensor_scan(out=cs, data0=rmask, data1=eb, initial=0.0, op0=ALU.mult, op1=ALU.add),
                  reads=(ebb, self.consts_b), writes=(csb,))
            kb.op("act", lambda e: e.activation(out=eb, in_=cs, func=AF.Exp), reads=(csb,), writes=(ebb,))
            kb.op("act", lambda e: e.activation(out=enb, in_=cs, func=AF.Exp, scale=-1.0), reads=(csb,), writes=(ebb,))
            kb.op("dve", lambda e: e.tensor_scalar(out=fz, in0=fz, scalar1=self.noml[:, hd:hd + 1], scalar2=self.oml[:, hd:hd + 1],
                                                   op0=ALU.mult, op1=ALU.add), reads=(fzb, self.spb), writes=(fzb,))
        if kind == "gla":
            def q_h(ci, mw, bank, bb):
                kb.op("dve", lambda e: e.scalar_tensor_tensor(out=qt, in0=bank[:, 0:T], scalar=128.0 ** -0.5, in1=eb,
                                                              op0=ALU.mult, op1=ALU.mult), reads=(bb, ebb), writes=(qtb,))
            self.proj_fm(W, oq, 128, xn_list, self.xnb, q_h)

            def k_h(ci, mw, bank, bb):
                kb.op("dve", lambda e: e.tensor_tensor(out=kt, in0=bank[:, 0:T], in1=enb, op=ALU.mult), reads=(bb, ebb), writes=(ktb,))
            self.proj_fm(W, ok, 128, xn_list, self.xnb, k_h)
        else:
            def q_h(ci, mw, bank, bb):
                tmp, tmpb = self.scratch()
                kb.op("act", lambda e: e.activation(out=tmp[:, 0:T], in_=bank[:, 0:T], func=AF.Silu), reads=(bb,), writes=(tmpb,))
                kb.op("dve", lambda e: e.tensor_tensor(out=qt, in0=tmp[:, 0:T], in1=eb, op=ALU.mult), reads=(tmpb, ebb), writes=(qtb,))
            self.proj_fm(W, oq, 128, xn_list, self.xnb, q_h)
            kb.op("dve", lambda e: e.tensor_tensor(out=kt, in0=fz, in1=enb, op=ALU.mult), reads=(fzb, ebb), writes=(ktb,))
        wv, wb = self.wload(W[:, ov:ov + dv], KC, dv)
        for n in range(nch):
            bank, bb = self.big_bank()
            for k in range(KC):
                self.mm(bank[0:32, 0:dv], self.xn[:, k, n * 32:(n + 1) * 32], wv[:, k, :], k == 0, k == KC - 1,
                        reads=(wb, self.xnb[k]), writes=(bb,), inc=(k == KC - 1))
            kb.op("act", lambda e, n=n, bank=bank: e.activation(out=v_tok[0:32, n, 0:dv], in_=bank[0:32, 0:dv], func=AF.Copy),
                  reads=(bb,), writes=(vtb,))
        for n0 in range(0, nch, 4):
            nn = min(4, nch - n0)
            for j in range(nn):
                n = n0 + j
                self.mm(b6[0:32, j * 128:(j + 1) * 128], kt[:, n * 32:(n + 1) * 32], self.ident_bf[:], True, True,
                        reads=(ktb, self.cbf_b), writes=(b6b,), inc=(j == nn - 1))
            kb.op("act", lambda e, n0=n0, nn=nn: e.activation(
                out=k_tok[0:32, n0:n0 + nn, :], in_=b6[0:32, 0:nn * 128].rearrange("p (n d) -> p n d", n=nn), func=AF.Copy),
                reads=(b6b,), writes=(ktkb,))
        for n in range(nch):
            self.mm(b6[0:32, n * 32:(n + 1) * 32], kt[:, n * 32:(n + 1) * 32], qt[:, n * 32:(n + 1) * 32], True, True,
                    reads=(ktb, qtb), writes=(b6b,), inc=(n == nch - 1))
        kb.op("dve", lambda e: e.tensor_tensor(out=AT[0:32, 0:nch * 32], in0=b6[0:32, 0:nch * 32], in1=maskT[:, 0:nch * 32], op=ALU.mult),
              reads=(b6b, self.consts_b), writes=(ATb,))
        def g_h(ci, mw, bank, bb):
            kb.op("act", lambda e: e.activation(out=sg[:, ci, :], in_=bank[:, 0:T], func=AF.Silu), reads=(bb,), writes=(sgb,))
        self.proj_fm(W, og, dv, xn_list, self.xnb, g_h)
        if kind == "gla" and li == 0 and hd == 0:
            self._dump_reads = [fzb, csb, ebb, qtb, ktb, ATb, vtb, ktkb]
            self.dump(0, fz); self.dump(1, cs); self.dump(2, eb); self.dump(3, enb); self.dump(4, qt); self.dump(5, kt)
            self.dump(6, AT[0:32, :], rows=32); self.dump(7, v_tok[0:32, 12, :], rows=32, cols=256); self.dump(8, k_tok[0:32, 12, :], rows=32, cols=128)
        kb.op("act", lambda e: e.activation(out=S_bf[:, 0:dv], in_=S, func=AF.Copy), reads=(Sb,), writes=(S_bfb,))
        for n in range(nch):
            if hs and n == npc:
                kb.dma(st_out[0, li, hd], S, reads=(Sb,), is_output=True)
                kb.dma(S, st_in[li, hd], writes=(Sb,))
                kb.op("act", lambda e: e.activation(out=S_bf[:, 0:dv], in_=S, func=AF.Copy), reads=(Sb,), writes=(S_bfb,))
            cols = slice(n * 32, (n + 1) * 32)
            for e_ in range(ndv):
                bank, bb = ob[e_]
                self.mm(bank[:, cols], v_tok[0:32, n, e_ * 128:(e_ + 1) * 128], AT[0:32, cols], True, False,
                        reads=(vtb, ATb), writes=(bb,), inc=False)
                self.mm(bank[:, cols], S_bf[:, e_ * 128:(e_ + 1) * 128], qt[:, cols], False, True,
                        reads=(S_bfb, qtb), writes=(bb,), inc=True)
            self.mm(b7[:, 0:dv], k_tok[0:32, n, :], v_tok[0:32, n, 0:dv], True, True,
                    reads=(ktkb, vtb), writes=(b7b,), inc=True)
            kb.op("dve", lambda e, n=n: e.tensor_scalar(out=S, in0=S, scalar1=eb[:, n * 32 + 31:n * 32 + 32], scalar2=None, op0=ALU.mult),
                  reads=(Sb, ebb), writes=(Sb,))
            kb.op("dve", lambda e, n=n: e.scalar_tensor_tensor(out=S, in0=b7[:, 0:dv], scalar=eb[:, n * 32 + 31:n * 32 + 32], in1=S,
                                                               op0=ALU.mult, op1=ALU.add), reads=(b7b, Sb, ebb), writes=(Sb,))
            kb.op("act", lambda e: e.activation(out=S_bf[:, 0:dv], in_=S, func=AF.Copy), reads=(Sb,), writes=(S_bfb,))
        if hs:
            kb.dma(st_out[1, li, hd], S, reads=(Sb,), is_output=True)
        elif last_tile:
            kb.dma(st_out[0, li, hd], S, reads=(Sb,), is_output=True)
        ssb, ssbb = self.big_bank()
        for e_ in range(ndv):
            bank, bb = ob[e_]
            sq, sqb = self.sq[e_ % 2], self.sqb[e_ % 2]
            kb.op("act", lambda e, bank=bank, sq=sq: e.activation(out=sq[:], in_=bank[:, 0:T], func=AF.Square), reads=(bb,), writes=(sqb,))
            self.mm(ssb[:, 0:T], self.ones_bf[:], sq[:], e_ == 0, e_ == ndv - 1, reads=(sqb, self.cbf_b), writes=(ssbb,), inc=True)
        rs, rsb = self.scratch()
        kb.op("act", lambda e: e.activation(out=rs[:, 0:T], in_=ssb[:, 0:T], func=AF.Sqrt, scale=1.0 / dv, bias=self.eps_col),
              reads=(ssbb, self.consts_b), writes=(rsb,))
        kb.op("dve", lambda e: e.reciprocal(out=rs[:, 0:T], in_=rs[:, 0:T]), reads=(rsb,), writes=(rsb,))
        for e_ in range(ndv):
            bank, bb = ob[e_]
            tmp, tmpb = self.scratch()
            kb.op("dve", lambda e, bank=bank, tmp=tmp, e_=e_: e.scalar_tensor_tensor(
                out=tmp[:, 0:T], in0=bank[:, 0:T], scalar=self.sp[:, ncol + e_:ncol + e_ + 1], in1=rs[:, 0:T], op0=ALU.mult, op1=ALU.mult),
                reads=(bb, rsb, self.spb), writes=(tmpb,))
            kb.op("dve", lambda e, tmp=tmp, e_=e_: e.tensor_tensor(out=self.mix[:, mix0 + e_, :], in0=tmp[:, 0:T], in1=sg[:, e_, :], op=ALU.mult),
                  reads=(tmpb, sgb), writes=(self.mixb[mix0 + e_],))


def fm_vec(v):
    v = np.asarray(v, dtype=np.float32)
    return np.ascontiguousarray(v.reshape(-1, 128).T)


def make_consts():
    c = np.zeros((128, NCONST), np.float32)
    c[:, 0:128] = np.eye(128, dtype=np.float32)
    c[:, 128:256] = 1.0
    c[:, 256] = EPS
    c[:, 257] = 1.0
    for n in range(NCH):
        for s_ in range(32):
            c[s_, 512 + n * 32 + s_:512 + (n + 1) * 32] = 1.0
    c[:, 928:928 + T] = 1.0
    c[:, 928:928 + T:32] = 0.0
    ii = np.arange(32)[:, None]
    jj = np.arange(32)[None, :]
    tri = (ii <= jj).astype(np.float32)
    c[0:32, 1344:1376] = tri
    c[0:32, 1376:1408] = (ii > jj).astype(np.float32)
    c[0:32, 1408:1440] = -1.0
    c[0:32, 1440:1504] = np.tile(np.where(ii >= jj, -30000.0, 0.0), (1, 2))
    c[0:32, 1504:1568] = np.tile(np.where(jj >= ii, -30000.0, 0.0), (1, 2))
    c[0:32, 1568:1632] = np.tile(np.where(ii > jj, -30000.0, 0.0), (1, 2))
    c[0:32, 1632:1696] = np.tile(np.eye(32, dtype=np.float32), (1, 2))
    c[0:32, 1696:1760] = np.tile(tri, (1, 2))
    return c


def build_program(cfg):
    nc = bass.Bass("TRN2", target_bir_lowering=False)
    b = Builder(nc, cfg)
    b.build()
    return nc


def prepare_inputs(inp, cfg):
    f = lambda a: np.ascontiguousarray(np.asarray(a, dtype=np.float32))
    xp = f(inp["x_prompt"])
    xs = f(inp["x_sample"])
    pp = f(inp["p_prompt"])
    ps = f(inp["p_sample"])
    vecs = np.concatenate([fm_vec(inp["norm_mix"][0]), fm_vec(inp["norm_mix"][1]),
                           fm_vec(inp["norm_ffn"][0]), fm_vec(inp["norm_ffn"][1]),
                           fm_vec(inp["norm_ple"][0]), fm_vec(inp["norm_ple"][1]),
                           fm_vec(inp["norm_final"])], axis=1)
    cw_ffn = np.stack([np.ascontiguousarray(f(inp["w_ffn_conv"][l]).T.reshape(172, 128, 3).transpose(1, 0, 2)).reshape(128, 172 * 3)
                       for l in range(DEPTH)])
    cw_gdn = np.stack([np.ascontiguousarray(f(inp["w_gdn_conv"][l]).T.reshape(48, 128, 4).transpose(1, 0, 2)).reshape(128, 48 * 4)
                       for l in range(DEPTH)])
    smallp = np.zeros((128, 64), np.float32)
    smallp[:, 0:4] = fm_vec(inp["b_gla_gate"][0]); smallp[:, 4:8] = fm_vec(inp["b_gla_gate"][1])
    smallp[:, 8:10] = fm_vec(inp["gla_norm"][0]); smallp[:, 10:12] = fm_vec(inp["gla_norm"][1])
    smallp[:, 12:13] = fm_vec(inp["gdn_norm"][0]); smallp[:, 13:14] = fm_vec(inp["gdn_norm"][1])
    smallp[:, 14:15] = fm_vec(inp["hgrn_norm"][0]); smallp[:, 15:16] = fm_vec(inp["hgrn_norm"][1])
    hlb = np.concatenate([fm_vec(inp["hgrn_lb"][0]), fm_vec(inp["hgrn_lb"][1])], axis=1)
    gdnrow = np.zeros((128, 832), np.float32)
    for l in range(DEPTH):
        gdnrow[:, l * 416:l * 416 + 208] = np.tile(f(inp["gdn_a_log"][l]), 13)[None, :]
        gdnrow[:, l * 416 + 208:l * 416 + 416] = np.tile(f(inp["gdn_dt_bias"][l]), 13)[None, :]
    consts = make_consts()
    shared = {
        "w_in": f(inp["w_in"]), "w_out": f(inp["w_out"]), "w_up": f(inp["w_up"]), "w_down": f(inp["w_down"]),
        "w_pg": f(inp["w_ple_gate"]), "w_pp": f(inp["w_ple_proj"]), "w_gg": f(inp["w_gla_gate"]),
        "vecs": vecs, "cw_ffn": cw_ffn, "cw_gdn": cw_gdn, "smallp": smallp, "hlb": hlb, "gdnrow": gdnrow,
        "consts": consts,
    }
    maps = []
    for c in range(8):
        b = c % 4
        xin = np.concatenate([xp[b].T, xs[c].T], axis=1)
        pin = np.concatenate([pp[:, b].transpose(0, 2, 1), ps[:, c].transpose(0, 2, 1)], axis=2)
        cg = f(inp["cache_gdn_conv"][:, c])
        cg = cg.transpose(0, 2, 1).reshape(DEPTH, 48, 128, 3).transpose(0, 2, 1, 3).reshape(DEPTH, 128, 48 * 3)
        cf = f(inp["cache_ffn_conv"][:, c])
        cf = cf.transpose(0, 2, 1).reshape(DEPTH, 172, 128, 2).transpose(0, 2, 1, 3).reshape(DEPTH, 128, 172 * 2)
        m = dict(shared)
        m.update({
            "xin": np.ascontiguousarray(xin), "pin": np.ascontiguousarray(pin),
            "st_gla": f(inp["state_gla"][:, c]), "st_gdn": f(inp["state_gdn"][:, c]), "st_hg": f(inp["state_hgrn"][:, c]),
            "cg_in": np.ascontiguousarray(cg), "cf_in": np.ascontiguousarray(cf),
        })
        maps.append(m)
    return maps


def assemble(results):
    y_p = np.stack([results[b]["yout"][:, 0:SEQ].T for b in range(4)])
    y_s = np.stack([results[c]["yout"][:, SEQ:NCOL].T for c in range(8)])

    def st(name, grp, n):
        return np.stack([results[c][name][grp] for c in range(n)], axis=1)

    def cache(name, grp, n, nchunk, w):
        outs = []
        for c in range(n):
            a = results[c][name][grp]
            a = a.reshape(DEPTH, 128, nchunk, w).transpose(0, 3, 2, 1).reshape(DEPTH, w, nchunk * 128)
            outs.append(a)
        return np.stack(outs, axis=1)
    out = (y_p, y_s,
           st("o_st_gla", 0, 4), st("o_st_gdn", 0, 4), cache("o_cg", 0, 4, 48, 3), st("o_st_hg", 0, 4), cache("o_cf", 0, 4, 172, 2),
           st("o_st_gla", 1, 8), st("o_st_gdn", 1, 8), cache("o_cg", 1, 8, 48, 3), st("o_st_hg", 1, 8), cache("o_cf", 1, 8, 172, 2))
    return tuple(np.ascontiguousarray(o.astype(np.float32)) for o in out)


def kernel(**inputs):
    nc = build_program(CFG)
    maps = prepare_inputs(inputs, CFG)
    res = run_bass_kernel_spmd(nc, maps, core_ids=list(range(8)))
    return assemble(res.results)
```
